# Optimizing a Trainium2 kernel written in Bass

```python
import math
import jax, jax.numpy as jnp
from jax import lax
import numpy as np

D_MODEL = 1024
BATCH = 16
SEQ = 2048
DEPTH = 2

N_MIXERS = 2
N_HEADS = 16
N_KV_HEADS = 4
HEAD_DIM = 64
GROUP = N_HEADS // N_KV_HEADS
Q_DIM = N_HEADS * HEAD_DIM
KV_DIM = N_KV_HEADS * HEAD_DIM
QKV_DIM = Q_DIM + 2 * KV_DIM
WINDOW = 128
BLOCK = 128
ROPE_THETA = 10000.0
CONV_WIDTH = 3
CONV_DIM = D_MODEL
D_FF = 2816
RMS_EPS = 1e-6
NEG_INF = -1e30
N_ATTN_LAYERS = (DEPTH + 1) // 2
N_CONV_LAYERS = DEPTH // 2

kernel_name = "hybrid_swa_shortconv_encoder"


def _rmsnorm(x, g):
    xf = x.astype(jnp.float32)
    y = xf * lax.rsqrt(jnp.mean(xf * xf, axis=-1, keepdims=True) + RMS_EPS)
    return (y * g.astype(jnp.float32)).astype(x.dtype)


def _dwconv3(x, w):
    xp = jnp.pad(x, ((0, 0), (1, 1), (0, 0)))
    return xp[:, :-2] * w[0] + xp[:, 1:-1] * w[1] + xp[:, 2:] * w[2]


def _rope_tables(positions, dtype):
    inv_freq = ROPE_THETA ** (-jnp.arange(0, HEAD_DIM, 2, dtype=jnp.float32) / HEAD_DIM)
    ang = positions.astype(jnp.float32)[:, None] * inv_freq[None, :]
    ang = jnp.concatenate([ang, ang], axis=-1)
    return jnp.cos(ang).astype(dtype), jnp.sin(ang).astype(dtype)


def _rotate_half(x):
    x1, x2 = jnp.split(x, 2, axis=-1)
    return jnp.concatenate([-x2, x1], axis=-1)


def _windowed_gqa(h, w_qkv, sink, w_o, cos, sin):
    B, S, _ = h.shape
    nb = S // BLOCK
    qkv = h @ w_qkv
    q = qkv[..., :Q_DIM].reshape(B, S, N_KV_HEADS, GROUP, HEAD_DIM)
    k = qkv[..., Q_DIM:Q_DIM + KV_DIM].reshape(B, S, N_KV_HEADS, HEAD_DIM)
    v = qkv[..., Q_DIM + KV_DIM:].reshape(B, S, N_KV_HEADS, HEAD_DIM)
    cq, sq = cos[None, :, None, None, :], sin[None, :, None, None, :]
    ck, sk = cos[None, :, None, :], sin[None, :, None, :]
    q = q * cq + _rotate_half(q) * sq
    k = k * ck + _rotate_half(k) * sk

    q_blk = q.reshape(B, nb, BLOCK, N_KV_HEADS, GROUP, HEAD_DIM).transpose(1, 0, 2, 3, 4, 5)

    def band(t):
        tp = jnp.pad(t, ((0, 0), (BLOCK, BLOCK), (0, 0), (0, 0)))
        tp = tp.reshape(B, nb + 2, BLOCK, N_KV_HEADS, HEAD_DIM)
        win = jnp.concatenate([tp[:, 0:nb], tp[:, 1:nb + 1], tp[:, 2:nb + 2]], axis=2)
        return win.transpose(1, 0, 2, 3, 4)

    k_win, v_win = band(k), band(v)
    r = jnp.arange(BLOCK)[:, None]
    t = jnp.arange(3 * BLOCK)[None, :]
    in_band = jnp.abs(t - BLOCK - r) <= WINDOW
    sink_f = sink.astype(jnp.float32).reshape(N_KV_HEADS, GROUP)[None, :, :, None, None]
    scale = 1.0 / math.sqrt(HEAD_DIM)

    def one_block(args):
        blk, qb, kb, vb = args
        k_pos = blk * BLOCK - BLOCK + t
        valid = in_band & (k_pos >= 0) & (k_pos < S)
        s = jnp.einsum('bqngd,bknd->bngqk', qb, kb).astype(jnp.float32) * scale
        s = jnp.where(valid[None, None, None], s, NEG_INF)
        m = jnp.maximum(jnp.max(s, axis=-1, keepdims=True), sink_f)
        p = jnp.exp(s - m)
        denom = jnp.sum(p, axis=-1, keepdims=True) + jnp.exp(sink_f - m)
        p = (p / denom).astype(vb.dtype)
        return jnp.einsum('bngqk,bknd->bqngd', p, vb)

    o = lax.map(one_block, (jnp.arange(nb), q_blk, k_win, v_win))
    o = o.transpose(1, 0, 2, 3, 4, 5).reshape(B, S, Q_DIM)
    return o @ w_o


def _short_conv_mixer(h, w_in, conv_w, w_out):
    bcx = h @ w_in
    b_gate, c_gate, xv = jnp.split(bcx, 3, axis=-1)
    y = b_gate * _dwconv3(c_gate * xv, conv_w)
    return y @ w_out


def _conv_glu_ffn(h, w_gate_up, conv_w, w_down):
    gu = h @ w_gate_up
    g, u = jnp.split(gu, 2, axis=-1)
    return (jax.nn.silu(_dwconv3(g, conv_w)) * u) @ w_down


def setup_inputs(seed: int = 0) -> dict:
    key = jax.random.key(seed)
    ks = jax.random.split(key, 12)
    f32 = jnp.float32

    def dense(k, shape, fan_in):
        return jax.random.normal(k, shape, f32) * fan_in ** -0.5

    return {
        "x": jax.random.normal(ks[0], (BATCH, SEQ, D_MODEL), f32),
        "positions": jnp.arange(SEQ, dtype=jnp.int32),
        "attn_w_qkv": dense(ks[1], (N_ATTN_LAYERS, D_MODEL, QKV_DIM), D_MODEL),
        "attn_sink": 0.5 * jax.random.normal(ks[2], (N_ATTN_LAYERS, N_HEADS), f32),
        "attn_w_o": dense(ks[3], (N_ATTN_LAYERS, Q_DIM, D_MODEL), Q_DIM),
        "conv_w_in": dense(ks[4], (N_CONV_LAYERS, D_MODEL, 3 * CONV_DIM), D_MODEL),
        "conv_w": dense(ks[5], (N_CONV_LAYERS, CONV_WIDTH, CONV_DIM), CONV_WIDTH),
        "conv_w_out": dense(ks[6], (N_CONV_LAYERS, CONV_DIM, D_MODEL), CONV_DIM),
        "norm_gains": 1.0 + 0.05 * jax.random.normal(ks[7], (DEPTH, 4, D_MODEL), f32),
        "ffn_w_gate_up": dense(ks[8], (DEPTH, D_MODEL, 2 * D_FF), D_MODEL),
        "ffn_conv_w": dense(ks[9], (DEPTH, CONV_WIDTH, D_FF), CONV_WIDTH),
        "ffn_w_down": dense(ks[10], (DEPTH, D_FF, D_MODEL), D_FF),
    }


def reference(x, positions, attn_w_qkv, attn_sink, attn_w_o, conv_w_in, conv_w, conv_w_out,
              norm_gains, ffn_w_gate_up, ffn_conv_w, ffn_w_down):
    cos, sin = _rope_tables(positions, x.dtype)
    h = x
    for i in range(DEPTH):
        g_pre_mix, g_post_mix, g_pre_ffn, g_post_ffn = (norm_gains[i, j] for j in range(4))
        hn = _rmsnorm(h, g_pre_mix)
        if i % N_MIXERS == 0:
            a = i // N_MIXERS
            mix = _windowed_gqa(hn, attn_w_qkv[a], attn_sink[a], attn_w_o[a], cos, sin)
        else:
            c = i // N_MIXERS
            mix = _short_conv_mixer(hn, conv_w_in[c], conv_w[c], conv_w_out[c])
        h = h + _rmsnorm(mix, g_post_mix)
        f = _conv_glu_ffn(_rmsnorm(h, g_pre_ffn), ffn_w_gate_up[i], ffn_conv_w[i], ffn_w_down[i])
        h = h + _rmsnorm(f, g_post_ffn)
    return h
```

```python
import contextlib
import math
import numpy as np
import concourse.bass as bass
import concourse.mybir as mybir
from concourse.bass_utils import run_bass_kernel_spmd
from concourse.alu_op_type import AluOpType as ALU

F32 = mybir.dt.float32
BF16 = mybir.dt.bfloat16
I32 = mybir.dt.int32
AF = mybir.ActivationFunctionType

D = 1024
S = 2048
NCH = 8
DFF = 2816
NFF = 22
NQB = 16
EPS = 1e-6
PI = math.pi
PI_S = 3.1415925
RING = 6

C_G = 0
C_CW = 64
C_FCW = 88
C_INVF = 220
C_SIGN = 221
C_PM = 222
C_MLO = 350
C_MHI = 478
C_TOT = 606


class Eng:
    def __init__(self, name, sem, is_pe=False):
        self.name = name
        self.sem = sem
        self.cnt = 0
        self.ops = []
        self.waited = {}
        self.is_pe = is_pe

    def _waits(self, waits):
        for (s, v) in waits:
            if self.is_pe and s is self.sem:
                continue
            k = id(s)
            if self.waited.get(k, 0) >= v:
                continue
            self.waited[k] = v
            self.ops.append(("w", s, v))

    def emit(self, fn, waits, inc=True):
        self._waits(waits)
        self.ops.append(("i", fn, inc))
        if inc:
            self.cnt += 1
            return (self.sem, self.cnt)
        return (self.sem, self.cnt + 1)

    def emit_dma(self, fn, waits, dsem):
        self._waits(waits)
        self.ops.append(("d", fn, dsem.sem))
        dsem.cnt += 16
        return (dsem.sem, dsem.cnt)

    def replay(self, e):
        for op in self.ops:
            if op[0] == "w":
                e.wait_ge(op[1], op[2])
            elif op[0] == "i":
                ins = op[1](e)
                if op[2]:
                    ins.then_inc(self.sem, 1)
            else:
                ins = op[1](e)
                ins.then_inc(op[2], 16)


class DSem:
    def __init__(self, sem):
        self.sem = sem
        self.cnt = 0


class Buf:
    __slots__ = ("w", "r", "excl")

    def __init__(self, excl=False):
        self.w = None
        self.r = {}
        self.excl = excl


def _deps(reads, writes, eng=None):
    ws = []
    for b in reads:
        if b.w is not None:
            ws.append(b.w)
        if b.excl:
            for t in b.r.values():
                if eng is None or t[0] is not eng.sem:
                    ws.append(t)
    for b in writes:
        ws.extend(b.r.values())
        if b.w is not None:
            ws.append(b.w)
    return ws


def _note(tok, reads, writes):
    for b in reads:
        k = id(tok[0])
        old = b.r.get(k)
        if old is None or old[1] < tok[1]:
            b.r[k] = tok
    for b in writes:
        b.w = tok
        b.r = {}


def op(eng, fn, reads=(), writes=(), inc=True):
    inc = True
    tok = eng.emit(fn, _deps(reads, writes, eng), inc)
    _note(tok, reads, writes)
    return tok


def dma(eng, fn, reads, writes, dsem):
    tok = eng.emit_dma(fn, _deps(reads, writes), dsem)
    _note(tok, reads, writes)
    return tok


class Pool_:
    def __init__(self, items):
        self.items = items
        self.i = 0

    def next(self):
        it = self.items[self.i % len(self.items)]
        self.i += 1
        return it


class Item:
    __slots__ = ("ap", "buf")

    def __init__(self, ap):
        self.ap = ap
        self.buf = Buf()


def f_mm(out, lhsT, rhs, start, stop):
    return lambda e: e.matmul(out, lhsT=lhsT, rhs=rhs, start=start, stop=stop)


def f_act(out, in_, func, scale=None, bias=None):
    kw = {}
    if scale is not None:
        kw["scale"] = scale
    if bias is not None:
        kw["bias"] = bias
    return lambda e: e.activation(out=out, in_=in_, func=func, **kw)


def f_tt(out, in0, in1, o):
    return lambda e: e.tensor_tensor(out=out, in0=in0, in1=in1, op=o)


def f_ts(out, in0, s1, s2, o0, o1=None):
    if o1 is None:
        return lambda e: e.tensor_scalar(out=out, in0=in0, scalar1=s1, scalar2=None, op0=o0)
    return lambda e: e.tensor_scalar(out=out, in0=in0, scalar1=s1, scalar2=s2, op0=o0, op1=o1)


def f_stt(out, in0, scalar, in1, o0, o1):
    return lambda e: e.scalar_tensor_tensor(out=out, in0=in0, scalar=scalar, in1=in1, op0=o0, op1=o1)


def f_copy(out, in_):
    return lambda e: e.tensor_copy(out=out, in_=in_)


def f_memset(ap, v):
    return lambda e: e.memset(ap, v)


def f_dma(out, in_):
    return lambda e: e.dma_start(out=out, in_=in_)


def build_program(n_seq=2, stop_after=None, dbg_cols=0):
    nc = bass.Bass("TRN2", target_bir_lowering=False)
    dr = {}

    def din(name, shape, dt=F32):
        dr[name] = nc.dram_tensor(name, list(shape), dt, kind="ExternalInput").ap()
        return dr[name]

    xT = din("xT", [n_seq, D, S])
    posd = din("pos", [128, S], I32)
    cstd = din("cst", [128, C_TOT])
    sinkd = din("sink", [1, 16])
    wqk = din("wqk", [12, 128, 1024])
    wv = din("wv", [2, 128, 1024])
    wo = din("wo", [8, 128, 1024])
    win = din("win", [24, 128, 1024])
    wout = din("wout", [8, 128, 1024])
    wgu = din("wgu", [2, 44, 128, 1024])
    wd = din("wd", [2, 8, 128, DFF])
    yT = nc.dram_tensor("yT", [n_seq, D, S], F32, kind="ExternalOutput").ap()
    dbg = None
    if dbg_cols:
        dbg = nc.dram_tensor("dbg", [128, dbg_cols], F32, kind="ExternalOutput").ap()

    SCRB = 112640
    with contextlib.ExitStack() as es:
        E = es.enter_context
        h = E(nc.sbuf_tensor("h", [128, NCH, S], F32))
        scr = E(nc.sbuf_tensor("scr", [128, SCRB // 4], F32))
        ring_t = [E(nc.sbuf_tensor(f"ring{i}", [128, 1024], BF16)) for i in range(RING)]
        cst = E(nc.sbuf_tensor("cst_sb", [128, C_TOT], F32))
        ones = E(nc.sbuf_tensor("ones", [128, 128], BF16))
        pm = E(nc.sbuf_tensor("pm", [128, 128], BF16))
        mlo4 = E(nc.sbuf_tensor("mlo4", [128, 512], BF16))
        mhi4 = E(nc.sbuf_tensor("mhi4", [128, 512], BF16))
        zo = E(nc.sbuf_tensor("zo", [128, 128], BF16))
        epst = E(nc.sbuf_tensor("epst", [128, 1], F32))
        sinks = E(nc.sbuf_tensor("sinks", [1, 16], F32))
        es16 = E(nc.sbuf_tensor("es16", [1, 16], F32))
        hsave = E(nc.sbuf_tensor("hsave", [128, 32], F32))
        sq_t = [E(nc.sbuf_tensor(f"sq{i}", [128, 512], BF16)) for i in range(4)]
        rs_t = [E(nc.sbuf_tensor(f"rs{i}", [128, 512], F32)) for i in range(2)]
        banks_t = [E(nc.psum_tensor(f"ps{i}", [128, 512], F32)) for i in range(8)]

        sem = lambda n: E(nc.semaphore(n))
        PE = Eng("pe", sem("s_pe"), is_pe=True)
        ACT = Eng("act", sem("s_act"))
        DVE = Eng("dve", sem("s_dve"))
        POOL = Eng("pool", sem("s_pool"))
        SP = Eng("sp", sem("s_sp"))
        ENGS = [PE, ACT, DVE, POOL, SP]
        all_dsems = []

        def new_dsem(n):
            d = DSem(sem(n))
            all_dsems.append(d)
            return d

        ring_items = []
        for i in range(RING):
            it = Item(ring_t[i])
            ring_items.append(it)
        ring_sems = [new_dsem(f"d_ring{i}") for i in range(RING)]
        ring_ctr = [0]

        def ring_load(src_ap, n):
            i = ring_ctr[0] % RING
            ring_ctr[0] += 1
            it = ring_items[i]
            dma(POOL, f_dma(it.ap[:, 0:n], src_ap), [], [it.buf], ring_sems[i])
            return it

        ld_sems = [[new_dsem(f"d_ld{c}_{hf}") for hf in range(2)] for c in range(NCH)]
        st_sems = [[new_dsem(f"d_st{c}_{hf}") for hf in range(2)] for c in range(NCH)]
        misc_sem = new_dsem("d_misc")
        dbg_sem = new_dsem("d_dbg")

        hB = [[Buf() for _ in range(4)] for _ in range(NCH)]
        cstB = Buf()
        hsaveB = Buf()
        constB = Buf()
        banks = [Item(banks_t[i]) for i in range(8)]
        for b_ in banks:
            b_.buf.excl = True
        poolA = Pool_(banks[0:6])
        poolB = Pool_(banks[6:8])
        sq_pool = Pool_([Item(t) for t in sq_t])
        rs_pool = Pool_([Item(t) for t in rs_t])

        def scr_f32(off, n):
            assert off % 4 == 0 and off + 4 * n <= SCRB, (off, n)
            return scr[:, off // 4: off // 4 + n]

        def scr_bf(off, n):
            assert off % 4 == 0 and n % 2 == 0 and off + 2 * n <= SCRB, (off, n)
            return scr[:, off // 4: off // 4 + n // 2].bitcast(BF16)

        def scr_i32(off, n):
            assert off % 4 == 0 and off + 4 * n <= SCRB
            return scr[:, off // 4: off // 4 + n].bitcast(I32)

        def barrier():
            toks = []
            for e_ in ENGS[:3]:
                if e_.cnt > 0:
                    toks.append((e_.sem, e_.cnt))
            for d in all_dsems:
                if d.cnt > 0:
                    toks.append((d.sem, d.cnt))
            for e_ in ENGS:
                e_._waits(toks)

        def gcol(l, j, c):
            return cst[:, C_G + (l * 4 + j) * 8 + c: C_G + (l * 4 + j) * 8 + c + 1]

        def norm_cols(src, srcB, gl, gj, dst, dstB, n):
            ssb = poolB.next()
            for c in range(NCH):
                sq = sq_pool.next()
                op(ACT, f_act(sq.ap[:, 0:n], src[c], AF.Square), [srcB[c]], [sq.buf])
                op(PE, f_mm(ssb.ap[:, 0:n], ones[:, :], sq.ap[:, 0:n], c == 0, c == NCH - 1),
                   [sq.buf, constB], [ssb.buf], inc=(c == NCH - 1))
            rs = rs_pool.next()
            op(ACT, f_act(rs.ap[:, 0:n], ssb.ap[:, 0:n], AF.Ln, scale=1.0 / D, bias=epst[:, 0:1]),
               [ssb.buf, constB], [rs.buf])
            op(ACT, f_act(rs.ap[:, 0:n], rs.ap[:, 0:n], AF.Exp, scale=-0.5), [rs.buf], [rs.buf])
            for c in range(NCH):
                op(DVE, f_stt(dst[c], src[c], gcol(gl, gj, c), rs.ap[:, 0:n], ALU.mult, ALU.mult),
                   [srcB[c], rs.buf, cstB], [dstB[c]])

        def proj_postnorm(units_fn, nk, mov, movB, xbuf, l, gj, st):
            s0 = st * 1024
            xv = [[xbuf[:, (m * 1024 + sub * 512):(m * 1024 + sub * 512 + 512)] for sub in range(2)]
                  for m in range(NCH)]
            xB = [[Buf() for _ in range(2)] for _ in range(NCH)]
            ssb = [banks[6], banks[7]]
            pend = []

            def flush(item):
                m, sqs = item
                for sub in range(2):
                    op(PE, f_mm(ssb[sub].ap[:, :], ones[:, :], sqs[sub].ap[:, :], m == 0, m == NCH - 1),
                       [sqs[sub].buf, constB], [ssb[sub].buf], inc=True)

            for m in range(NCH):
                slots = units_fn(m)
                fb = [poolA.next(), poolA.next()]
                k = 0
                for (slot, nkk) in slots:
                    for kk in range(nkk):
                        for sub in range(2):
                            op(PE, f_mm(fb[sub].ap[:, :], slot.ap[:, kk * 128:(kk + 1) * 128],
                                        mov[k][:, sub * 512:(sub + 1) * 512], k == 0, k == nk - 1),
                               [slot.buf, movB[k]], [fb[sub].buf], inc=(k == nk - 1))
                        k += 1
                sqs = []
                for sub in range(2):
                    sq = sq_pool.next()
                    op(ACT, f_act(sq.ap[:, :], fb[sub].ap[:, :], AF.Square), [fb[sub].buf], [sq.buf])
                    op(ACT, f_act(xv[m][sub], fb[sub].ap[:, :], AF.Copy, scale=gcol(l, gj, m)),
                       [fb[sub].buf, cstB], [xB[m][sub]])
                    sqs.append(sq)
                pend.append((m, sqs))
                if len(pend) > 1:
                    flush(pend.pop(0))
            while pend:
                flush(pend.pop(0))
            for sub in range(2):
                rs = rs_pool.next()
                op(ACT, f_act(rs.ap[:, :], ssb[sub].ap[:, :], AF.Ln, scale=1.0 / D, bias=epst[:, 0:1]),
                   [ssb[sub].buf, constB], [rs.buf])
                op(ACT, f_act(rs.ap[:, :], rs.ap[:, :], AF.Exp, scale=-0.5), [rs.buf], [rs.buf])
                tt = 2 * st + sub
                for m in range(NCH):
                    hv = h[:, m, s0 + sub * 512: s0 + sub * 512 + 512]
                    op(DVE, f_tt(xv[m][sub], xv[m][sub], rs.ap[:, :], ALU.mult), [xB[m][sub], rs.buf], [xB[m][sub]])
                    op(DVE, f_tt(hv, hv, xv[m][sub], ALU.add), [xB[m][sub], hB[m][tt]], [hB[m][tt]])

        def norm_super(l, gj, st, hn, hnB):
            s0 = st * 1024
            for sub in range(2):
                tt = 2 * st + sub
                src = [h[:, c, s0 + sub * 512: s0 + sub * 512 + 512] for c in range(NCH)]
                dst = [hn[c][:, 1 + sub * 512: 1 + sub * 512 + 512] for c in range(NCH)]
                norm_cols(src, [hB[c][tt] for c in range(NCH)], l, gj, dst, hnB, 512)
            if st == 0:
                tok, hc, tt = 1024, 1025, 2
                src = [h[:, c, tok:tok + 1] for c in range(NCH)]
                dst = [hn[c][:, hc:hc + 1] for c in range(NCH)]
                norm_cols(src, [hB[c][tt] for c in range(NCH)], l, gj, dst, hnB, 1)
            else:
                hc = 0
            return hc

        halo_slots = [Item(banks[6].ap[:, i:i + 1]) for i in range(8)]
        for hs in halo_slots:
            hs.buf = banks[6].buf
        halo_ctr = [0]

        def proj_halo(slot, hn, hnB, hc, with_halo=True):
            b0, b1 = poolA.next(), poolA.next()
            hi = None
            if with_halo:
                hi = halo_slots[halo_ctr[0] % 8]
                halo_ctr[0] += 1
            for k in range(NCH):
                w = slot.ap[:, k * 128:(k + 1) * 128]
                op(PE, f_mm(b0.ap[:, :], w, hn[k][:, 1:513], k == 0, k == 7), [slot.buf, hnB[k]], [b0.buf],
                   inc=(k == 7))
                op(PE, f_mm(b1.ap[:, :], w, hn[k][:, 513:1025], k == 0, k == 7), [slot.buf, hnB[k]], [b1.buf],
                   inc=(k == 7))
                if with_halo:
                    op(PE, f_mm(hi.ap, w, hn[k][:, hc:hc + 1], k == 0, k == 7), [slot.buf, hnB[k]], [hi.buf],
                       inc=(k == 7))
            return b0, b1, hi

        def conv3(dst, src, wcol, srcB, dstB):
            op(DVE, f_ts(dst, src[:, 1:1025], wcol(1), None, ALU.mult), [srcB, cstB], [dstB])
            op(DVE, f_stt(dst, src[:, 0:1024], wcol(0), dst, ALU.mult, ALU.add), [srcB, dstB, cstB], [dstB])
            op(DVE, f_stt(dst, src[:, 2:1026], wcol(2), dst, ALU.mult, ALU.add), [srcB, dstB, cstB], [dstB])

        def ffn_phase(l):
            for st in range(2):
                barrier()
                hn = [scr_bf(c * 2052, 1026) for c in range(NCH)]
                hnB = [Buf() for _ in range(NCH)]
                xbuf = scr_f32(0, 8192)
                A0 = 32832
                actv = [scr_bf(A0 + j * 2048, 1024) for j in range(NFF)]
                actB = [Buf() for _ in range(NFF)]
                G0 = A0 + NFF * 2048
                gs_pool = Pool_([Item(scr_f32(G0 + i * 4112, 1026)) for i in range(2)])
                G1 = G0 + 2 * 4112
                gc_pool = Pool_([Item(scr_f32(G1 + i * 4096, 1024)) for i in range(2)])
                G2 = G1 + 2 * 4096
                sl_pool = Pool_([Item(scr_bf(G2 + i * 2048, 1024)) for i in range(2)])
                hc = norm_super(l, 2, st, hn, hnB)
                zc = 1025 - hc
                for j in range(NFF):
                    sg = ring_load(wgu[l, 2 * j], 1024)
                    su = ring_load(wgu[l, 2 * j + 1], 1024)
                    g0, g1, gh = proj_halo(sg, hn, hnB, hc, with_halo=(st == 0))
                    u0, u1, _ = proj_halo(su, hn, hnB, hc, with_halo=False)
                    gs = gs_pool.next()
                    op(ACT, f_act(gs.ap[:, 1:513], g0.ap[:, :], AF.Copy), [g0.buf], [gs.buf])
                    op(ACT, f_act(gs.ap[:, 513:1025], g1.ap[:, :], AF.Copy), [g1.buf], [gs.buf])
                    if st == 0:
                        op(ACT, f_act(gs.ap[:, hc:hc + 1], gh.ap, AF.Copy), [gh.buf], [gs.buf])
                        op(ACT, f_act(hsave[:, j:j + 1], g1.ap[:, 511:512], AF.Copy), [g1.buf], [hsaveB])
                    else:
                        op(ACT, f_act(gs.ap[:, 0:1], hsave[:, j:j + 1], AF.Copy), [hsaveB], [gs.buf])
                    op(DVE, f_memset(gs.ap[:, zc:zc + 1], 0.0), [], [gs.buf])
                    gc = gc_pool.next()
                    conv3(gc.ap, gs.ap, lambda jj, j=j: cst[:, C_FCW + (l * 3 + jj) * 22 + j: C_FCW + (l * 3 + jj) * 22 + j + 1],
                          gs.buf, gc.buf)
                    sl = sl_pool.next()
                    op(ACT, f_act(sl.ap[:, :], gc.ap[:, :], AF.Silu), [gc.buf], [sl.buf])
                    op(DVE, f_tt(actv[j][:, 0:512], sl.ap[:, 0:512], u0.ap[:, :], ALU.mult), [sl.buf, u0.buf], [actB[j]])
                    op(DVE, f_tt(actv[j][:, 512:1024], sl.ap[:, 512:1024], u1.ap[:, :], ALU.mult), [sl.buf, u1.buf], [actB[j]])

                def units_fn(m):
                    r = []
                    for (k0, nkk) in ((0, 8), (8, 8), (16, 6)):
                        r.append((ring_load(wd[l, m, :, k0 * 128:(k0 + nkk) * 128], nkk * 128), nkk))
                    return r
                proj_postnorm(units_fn, NFF, actv, actB, xbuf, l, 3, st)

        def attn_layer(seq):
            l = 0
            Q0, K0, V0, O0 = 0, 32768, 49152, 65536
            qrot = [scr_bf(Q0 + c * 4096, 2048) for c in range(8)]
            krot = [scr_bf(K0 + n * 4096, 2048) for n in range(4)]
            vaug_all = scr_bf(V0, 8192)
            qB = [[Buf() for _ in range(4)] for _ in range(8)]
            kB = [[Buf() for _ in range(4)] for _ in range(4)]
            vB = [Buf() for _ in range(16)]

            def vaug(blk, n):
                o = (blk * 4 + n) * 128
                return vaug_all[:, o:o + 128]

            barrier()
            if stop_after == 'startup':
                return {}
            X = 65536
            cosT = scr_f32(X, 2048)
            sinT = scr_f32(X + 8192, 2048)
            hn_ap = [scr_bf(X + 16384 + c * 1024, 512) for c in range(8)]
            U0 = X + 24576
            posi = scr_i32(U0, 2048)
            R0 = U0 + 8192
            ang = Item(scr_f32(R0, 512))
            ang2 = Item(scr_f32(R0 + 2048, 512))
            uu = Item(scr_f32(R0 + 4096, 512))
            ki = Item(scr_i32(R0 + 6144, 512))
            qb_pool = Pool_([Item(scr_bf(U0 + i * 1024, 512)) for i in range(2)])
            t1_pool = Pool_([Item(scr_f32(U0 + 2048 + i * 2048, 512)) for i in range(2)])
            t2_pool = Pool_([Item(scr_f32(U0 + 6144 + i * 2048, 512)) for i in range(2)])
            tabB = Buf()
            posB = Buf()
            dma(SP, f_dma(posi, posd[:, :]), [], [posB], misc_sem)
            v3 = vaug_all.rearrange("p (b c) -> p b c", c=128)
            op(DVE, f_memset(v3[:, :, 64:128], 1.0), [], vB)
            invf = cst[:, C_INVF:C_INVF + 1]
            sgn = cst[:, C_SIGN:C_SIGN + 1]
            for pc in range(4):
                cs = slice(pc * 512, (pc + 1) * 512)
                op(DVE, f_ts(ang.ap, posi[:, cs], invf, None, ALU.mult), [posB, cstB], [ang.buf])
                for (off, dstT, scl) in ((0.0, sinT, sgn), (PI / 2, cosT, None)):
                    op(DVE, f_ts(ang2.ap, ang.ap, off, None, ALU.add), [ang.buf], [ang2.buf])
                    op(DVE, f_ts(uu.ap, ang2.ap, 1.0 / (2 * PI), None, ALU.mult), [ang2.buf], [uu.buf])
                    op(DVE, f_copy(ki.ap, uu.ap), [uu.buf], [ki.buf])
                    op(DVE, f_copy(uu.ap, ki.ap), [ki.buf], [uu.buf])
                    op(DVE, f_stt(ang2.ap, uu.ap, -2 * PI, ang2.ap, ALU.mult, ALU.add), [uu.buf, ang2.buf], [ang2.buf])
                    op(DVE, f_ts(uu.ap, ang2.ap, PI, None, ALU.is_gt), [ang2.buf], [uu.buf])
                    op(DVE, f_stt(ang2.ap, uu.ap, -2 * PI, ang2.ap, ALU.mult, ALU.add), [uu.buf, ang2.buf], [ang2.buf])
                    op(DVE, f_ts(uu.ap, ang2.ap, -PI, None, ALU.is_lt), [ang2.buf], [uu.buf])
                    op(DVE, f_stt(ang2.ap, uu.ap, 2 * PI, ang2.ap, ALU.mult, ALU.add), [uu.buf, ang2.buf], [ang2.buf])
                    op(DVE, f_ts(ang2.ap, ang2.ap, -PI_S, PI_S, ALU.max, ALU.min), [ang2.buf], [ang2.buf])
                    if scl is None:
                        op(ACT, f_act(dstT[:, cs], ang2.ap, AF.Sin), [ang2.buf], [tabB])
                    else:
                        op(ACT, f_act(dstT[:, cs], ang2.ap, AF.Sin, scale=scl), [ang2.buf, cstB], [tabB])

            barrier()
            if stop_after == 'rope':
                return dict(cos=cosT, sin=sinT)
            hnB = [Buf() for _ in range(8)]
            for tt in range(4):
                ts_ = slice(tt * 512, (tt + 1) * 512)
                src = [h[:, c, ts_] for c in range(8)]
                norm_cols(src, [hB[c][tt] for c in range(8)], 0, 0, hn_ap, hnB, 512)
                if stop_after == 'norm':
                    return dict(hn=hn_ap)
                for m in range(12):
                    slot = ring_load(wqk[m], 1024)
                    bk = poolA.next()
                    for k in range(8):
                        op(PE, f_mm(bk.ap[:, :], slot.ap[:, k * 128:(k + 1) * 128], hn_ap[k], k == 0, k == 7),
                           [slot.buf, hnB[k]], [bk.buf], inc=(k == 7))
                    qb = qb_pool.next()
                    op(ACT, f_act(qb.ap, bk.ap[:, :], AF.Copy), [bk.buf], [qb.buf])
                    t1 = t1_pool.next()
                    op(DVE, f_tt(t1.ap, bk.ap[:, :], cosT[:, ts_], ALU.mult), [bk.buf, tabB], [t1.buf])
                    sw = poolB.next()
                    op(PE, f_mm(sw.ap[:, :], pm[:, :], qb.ap, True, True), [qb.buf, constB], [sw.buf])
                    t2 = t2_pool.next()
                    op(DVE, f_tt(t2.ap, sw.ap[:, :], sinT[:, ts_], ALU.mult), [sw.buf, tabB], [t2.buf])
                    if m < 8:
                        dst, dB = qrot[m][:, ts_], qB[m][tt]
                    else:
                        dst, dB = krot[m - 8][:, ts_], kB[m - 8][tt]
                    op(DVE, f_tt(dst, t1.ap, t2.ap, ALU.add), [t1.buf, t2.buf], [dB])
                if stop_after == 'qk':
                    return dict(q0=[qrot[c][:, 0:512] for c in range(8)])
                v0 = ring_load(wv[0], 1024)
                v1 = ring_load(wv[1], 1024)
                for b4 in range(4):
                    blk = tt * 4 + b4
                    bk = poolA.next()
                    for k in range(8):
                        vs = v0 if k < 4 else v1
                        op(PE, f_mm(bk.ap[:, 0:256], hn_ap[k][:, b4 * 128:(b4 + 1) * 128],
                                    vs.ap[:, (k % 4) * 256:(k % 4) * 256 + 256], k == 0, k == 7),
                           [vs.buf, hnB[k]], [bk.buf], inc=(k == 7))
                    dstv = v3[:, blk * 4:(blk + 1) * 4, 0:64]
                    srcv = bk.ap[:, 0:256].rearrange("p (n d) -> p n d", d=64)
                    op(ACT, f_act(dstv, srcv, AF.Copy), [bk.buf], [vB[blk]])
            if stop_after == "qkv":
                return dict(q=qrot, k=krot, v=vaug_all, cos=cosT, sin=sinT)

            barrier()
            ohat = [scr_bf(O0 + c * 4096, 2048) for c in range(8)]
            oB = [[Buf() for _ in range(16)] for _ in range(8)]
            Y = 98304
            pt_pool = Pool_([Item(scr_bf(Y + i * 3072, 1536)) for i in range(2)])
            lr_pool = Pool_([Item(scr_f32(Y + 6144 + i * 2048, 512)) for i in range(2)])
            es_row = scr_bf(Y + 10240, 2048)
            esB = Buf()
            op(DVE, f_memset(es_row[:, :], 0.0), [], [esB])
            op(ACT, f_act(es16[0:1, :], sinks[0:1, :], AF.Exp), [constB], [esB])
            for hh in range(16):
                op(DVE, f_ts(es_row[0:1, hh * 128:(hh + 1) * 128], ones[0:1, 0:128], es16[0:1, hh:hh + 1], None, ALU.mult),
                   [esB, constB], [esB])
            ohat3 = scr_bf(O0, 16384).rearrange("p (c t) -> p c t", c=8)
            for i in range(NQB):
                for n in range(4):
                    kbs = [kb for kb in (i - 1, i, i + 1) if 0 <= kb < NQB]
                    ptall = pt_pool.next()
                    pt5 = ptall.ap.rearrange("p (k a b q) -> p k a b q", k=3, a=2, b=2)
                    bank_of = {}
                    for b in range(2):
                        bank_of[(b, 0)] = poolA.next()
                        if len(kbs) == 3:
                            bank_of[(b, 1)] = poolA.next()
                    for idx, kb in enumerate(kbs):
                        for g in range(4):
                            hd = 4 * n + g
                            c = hd // 2
                            b = hd % 2
                            lo = b * 64
                            bk = bank_of[(b, idx // 2)]
                            col = ((idx % 2) * 2 + g // 2) * 128
                            op(PE, f_mm(bk.ap[:, col:col + 128],
                                        krot[n][lo:lo + 64, kb * 128:(kb + 1) * 128],
                                        qrot[c][lo:lo + 64, i * 128:(i + 1) * 128], True, True),
                               [kB[n][kb // 4], qB[c][i // 4]], [bk.buf])
                    pts = []
                    for idx, kb in enumerate(kbs):
                        for b in range(2):
                            bk = bank_of[(b, idx // 2)]
                            c0 = (idx % 2) * 256
                            src = bk.ap[:, c0:c0 + 256].rearrange("p (a q) -> p a q", a=2)
                            op(ACT, f_act(pt5[:, idx, :, b, :], src, AF.Exp, scale=0.125), [bk.buf], [ptall.buf])
                        pt_ap = ptall.ap[:, idx * 512:(idx + 1) * 512]
                        if kb == i - 1:
                            op(DVE, f_tt(pt_ap, pt_ap, mlo4[:, :], ALU.mult), [ptall.buf, constB], [ptall.buf])
                        elif kb == i + 1:
                            op(DVE, f_tt(pt_ap, pt_ap, mhi4[:, :], ALU.mult), [ptall.buf, constB], [ptall.buf])
                        pts.append((kb, pt_ap))
                    od = poolB.next()
                    for idx, (kb, pt_ap) in enumerate(pts):
                        op(PE, f_mm(od.ap[:, :], vaug(kb, n), pt_ap, idx == 0, False), [vB[kb], ptall.buf], [od.buf], inc=False)
                    op(PE, f_mm(od.ap[:, :], zo[:, :], es_row[:, n * 512:(n + 1) * 512], False, True),
                       [esB, constB], [od.buf], inc=True)
                    lr = lr_pool.next()
                    op(ACT, f_act(lr.ap[64:128, :], od.ap[64:128, :], AF.Ln), [od.buf], [lr.buf])
                    op(ACT, f_act(lr.ap[64:128, :], lr.ap[64:128, :], AF.Exp, scale=-1.0), [lr.buf], [lr.buf])
                    o4 = od.ap[0:64, :].rearrange("p (a b q) -> p a b q", a=2, b=2)
                    r4 = lr.ap[64:128, :].rearrange("p (a b q) -> p a b q", a=2, b=2)
                    for b in range(2):
                        dsto = ohat3[b * 64:(b + 1) * 64, 2 * n:2 * n + 2, i * 128:(i + 1) * 128]
                        op(DVE, f_tt(dsto, o4[:, :, b, :], r4[:, :, b, :], ALU.mult), [od.buf, lr.buf],
                           [oB[2 * n][i], oB[2 * n + 1][i]])
            if stop_after == "attn":
                return dict(o=ohat)

            barrier()
            xbuf = scr_f32(0, 8192)
            for st in range(2):
                mov = [ohat[k][:, st * 1024:(st + 1) * 1024] for k in range(8)]
                movB = []
                for k in range(8):
                    bb = Buf()
                    toks = [oB[k][i].w for i in range(st * 8, st * 8 + 8)]
                    bb.w = max(toks, key=lambda t: t[1])
                    movB.append(bb)
                proj_postnorm(lambda m: [(ring_load(wo[m], 1024), 8)], 8, mov, movB, xbuf, 0, 1, st)
                if st == 0:
                    barrier()
            return None

        def conv_layer():
            l = 1
            for st in range(2):
                barrier()
                hn = [scr_bf(c * 2052, 1026) for c in range(NCH)]
                hnB = [Buf() for _ in range(NCH)]
                A0 = 32832
                yb = [scr_bf(A0 + m * 2048, 1024) for m in range(8)]
                ybB = [Buf() for _ in range(8)]
                T0 = A0 + 16384
                xs_pool = Pool_([Item(scr_f32(T0 + i * 4112, 1026)) for i in range(2)])
                cx_pool = Pool_([Item(scr_f32(T0 + 8224 + i * 4112, 1026)) for i in range(2)])
                y_pool = Pool_([Item(scr_f32(T0 + 16448 + i * 4096, 1024)) for i in range(2)])
                hc = norm_super(l, 0, st, hn, hnB)
                zc = 1025 - hc
                for m in range(8):
                    sx = ring_load(win[16 + m], 1024)
                    sc = ring_load(win[8 + m], 1024)
                    sb = ring_load(win[m], 1024)
                    x0, x1, xh = proj_halo(sx, hn, hnB, hc, with_halo=(st == 0))
                    xs = xs_pool.next()
                    op(ACT, f_act(xs.ap[:, 1:513], x0.ap[:, :], AF.Copy), [x0.buf], [xs.buf])
                    op(ACT, f_act(xs.ap[:, 513:1025], x1.ap[:, :], AF.Copy), [x1.buf], [xs.buf])
                    if st == 0:
                        op(ACT, f_act(xs.ap[:, hc:hc + 1], xh.ap, AF.Copy), [xh.buf], [xs.buf])
                    c0, c1, ch = proj_halo(sc, hn, hnB, hc, with_halo=(st == 0))
                    cx = cx_pool.next()
                    op(DVE, f_tt(cx.ap[:, 1:513], c0.ap[:, :], xs.ap[:, 1:513], ALU.mult), [c0.buf, xs.buf], [cx.buf])
                    op(DVE, f_tt(cx.ap[:, 513:1025], c1.ap[:, :], xs.ap[:, 513:1025], ALU.mult), [c1.buf, xs.buf], [cx.buf])
                    if st == 0:
                        op(DVE, f_tt(cx.ap[:, hc:hc + 1], ch.ap, xs.ap[:, hc:hc + 1], ALU.mult), [ch.buf, xs.buf], [cx.buf])
                        op(DVE, f_copy(hsave[:, 24 + m:25 + m], cx.ap[:, 1024:1025]), [cx.buf], [hsaveB])
                    else:
                        op(DVE, f_copy(cx.ap[:, 0:1], hsave[:, 24 + m:25 + m]), [hsaveB], [cx.buf])
                    op(DVE, f_memset(cx.ap[:, zc:zc + 1], 0.0), [], [cx.buf])
                    y = y_pool.next()
                    conv3(y.ap, cx.ap, lambda jj, m=m: cst[:, C_CW + jj * 8 + m: C_CW + jj * 8 + m + 1], cx.buf, y.buf)
                    b0, b1, _ = proj_halo(sb, hn, hnB, hc, with_halo=False)
                    op(DVE, f_tt(yb[m][:, 0:512], y.ap[:, 0:512], b0.ap[:, :], ALU.mult), [y.buf, b0.buf], [ybB[m]])
                    op(DVE, f_tt(yb[m][:, 512:1024], y.ap[:, 512:1024], b1.ap[:, :], ALU.mult), [y.buf, b1.buf], [ybB[m]])
                xbuf = scr_f32(0, 8192)
                proj_postnorm(lambda m: [(ring_load(wout[m], 1024), 8)], 8, yb, ybB, xbuf, 1, 1, st)

        dma(SP, f_dma(cst[:, :], cstd[:, :]), [], [cstB], misc_sem)
        dma(SP, f_dma(sinks[0:1, :], sinkd[:, :]), [], [constB], misc_sem)
        op(DVE, f_memset(ones[:, :], 1.0), [], [constB])
        op(DVE, f_memset(epst[:, :], EPS), [], [constB])
        op(DVE, f_memset(zo[:, :], 0.0), [], [constB])
        op(DVE, f_memset(zo[0:1, 64:128], 1.0), [constB], [constB])
        op(DVE, f_copy(pm[:, :], cst[:, C_PM:C_PM + 128]), [cstB], [constB])
        for g in range(4):
            op(DVE, f_copy(mlo4[:, g * 128:(g + 1) * 128], cst[:, C_MLO:C_MLO + 128]), [cstB], [constB])
            op(DVE, f_copy(mhi4[:, g * 128:(g + 1) * 128], cst[:, C_MHI:C_MHI + 128]), [cstB], [constB])

        dump = None
        for seq in range(n_seq):
            for c in range(NCH):
                for hf in range(2):
                    dma(SP, f_dma(h[:, c, hf * 1024:(hf + 1) * 1024], xT[seq, c * 128:(c + 1) * 128, hf * 1024:(hf + 1) * 1024]),
                        [], [hB[c][2 * hf], hB[c][2 * hf + 1]], ld_sems[c][hf])
            dump = attn_layer(seq)
            if dump is not None:
                pass
            elif stop_after == "l0mix":
                pass
            else:
                ffn_phase(0)
                if stop_after != "l0":
                    conv_layer()
                    if stop_after != "l1mix":
                        ffn_phase(1)
            for c in range(NCH):
                for hf in range(2):
                    dma(SP, f_dma(yT[seq, c * 128:(c + 1) * 128, hf * 1024:(hf + 1) * 1024], h[:, c, hf * 1024:(hf + 1) * 1024]),
                        [hB[c][2 * hf], hB[c][2 * hf + 1]], [], st_sems[c][hf])

        if dump and dbg is not None:
            barrier()
            col = 0
            for name, aps in dump.items():
                aps = aps if isinstance(aps, list) else [aps]
                for a0 in aps:
                    for c0 in range(0, a0.shape[-1], 2048):
                        a = a0[:, c0:min(c0 + 2048, a0.shape[-1])]
                        n = a.shape[-1]
                        stg = scr_f32(SCRB - 8192, 2048)[:, 0:n]
                        bb = Buf()
                        op(DVE, f_copy(stg, a), [], [bb])
                        dma(SP, f_dma(dbg[:, col:col + n], stg), [bb], [], dbg_sem)
                        barrier()
                        col += n
        barrier()

        with nc.Block() as block:
            @block.tensor
            def _(e):
                PE.replay(e)

            @block.scalar
            def _(e):
                ACT.replay(e)

            @block.vector
            def _(e):
                DVE.replay(e)

            @block.gpsimd
            def _(e):
                POOL.replay(e)

            @block.sync
            def _(e):
                SP.replay(e)
    return nc


def _units(W, cols_list):
    K = W.shape[0]
    nk = K // 128
    Wr = W.reshape(nk, 128, W.shape[1])
    out = np.empty((len(cols_list), 128, nk * 128), np.float32)
    for i, ci in enumerate(cols_list):
        out[i] = Wr[:, :, ci].transpose(1, 0, 2).reshape(128, nk * 128)
    return out


def prep_shared(inp):
    f = lambda a: np.asarray(a, dtype=np.float32)
    wqkv = f(inp["attn_w_qkv"])[0]
    ar = np.arange(128)
    cols = [m * 128 + ar for m in range(8)] + [1024 + n * 64 + (ar % 64) for n in range(4)]
    sh = {}
    sh["wqk"] = _units(wqkv, cols)
    Wr = wqkv.reshape(8, 128, 1536)
    sh["wv"] = np.stack([Wr[4 * u:4 * u + 4, :, 1280:1536].transpose(1, 0, 2).reshape(128, 1024) for u in range(2)])
    sh["wo"] = _units(f(inp["attn_w_o"])[0], [m * 128 + ar for m in range(8)])
    sh["win"] = _units(f(inp["conv_w_in"])[0], [m * 128 + ar for m in range(24)])
    sh["wout"] = _units(f(inp["conv_w_out"])[0], [m * 128 + ar for m in range(8)])
    gu = f(inp["ffn_w_gate_up"])
    cols = []
    for j in range(NFF):
        cols.append(j * 128 + ar)
        cols.append(DFF + j * 128 + ar)
    sh["wgu"] = np.stack([_units(gu[l], cols) for l in range(2)])
    wdn = f(inp["ffn_w_down"])
    sh["wd"] = np.stack([_units(wdn[l], [m * 128 + ar for m in range(8)]) for l in range(2)])
    cst = np.zeros((128, C_TOT), np.float32)
    ng = f(inp["norm_gains"])
    for l in range(2):
        for j in range(4):
            cst[:, C_G + (l * 4 + j) * 8: C_G + (l * 4 + j) * 8 + 8] = ng[l, j].reshape(8, 128).T
    cw = f(inp["conv_w"])[0]
    for j in range(3):
        cst[:, C_CW + j * 8: C_CW + j * 8 + 8] = cw[j].reshape(8, 128).T
    fcw = f(inp["ffn_conv_w"])
    for l in range(2):
        for j in range(3):
            cst[:, C_FCW + (l * 3 + j) * 22: C_FCW + (l * 3 + j) * 22 + 22] = fcw[l, j].reshape(22, 128).T
    inv_freq = (10000.0 ** (-np.arange(0, 64, 2, dtype=np.float32) / np.float32(64))).astype(np.float32)
    p = np.arange(128)
    d = p % 64
    cst[:, C_INVF] = inv_freq[d % 32]
    cst[:, C_SIGN] = np.where(d < 32, -1.0, 1.0)
    partner = np.where(d < 32, p + 32, p - 32)
    cst[partner, C_PM + p] = 1.0
    r = p[:, None]
    t = p[None, :]
    cst[:, C_MLO:C_MLO + 128] = (t <= r).astype(np.float32)
    cst[:, C_MHI:C_MHI + 128] = (r <= t).astype(np.float32)
    sh["cst"] = cst
    sh["sink"] = f(inp["attn_sink"]).reshape(1, 16)
    sh["pos"] = np.ascontiguousarray(np.broadcast_to(np.asarray(inp["positions"], dtype=np.int32)[None, :], (128, S)))
    return sh


_NC_CACHE = {}


def kernel(**inputs):
    x = np.asarray(inputs["x"], dtype=np.float32)
    sh = prep_shared(inputs)
    n_cores = 8
    per = x.shape[0] // n_cores
    if "nc" not in _NC_CACHE:
        _NC_CACHE["nc"] = build_program(n_seq=per)
    nc = _NC_CACHE["nc"]
    in_maps = []
    for c in range(n_cores):
        m = dict(sh)
        m["xT"] = np.ascontiguousarray(x[c * per:(c + 1) * per].transpose(0, 2, 1))
        in_maps.append(m)
    res = run_bass_kernel_spmd(nc, in_maps, core_ids=list(range(n_cores)))
    out = np.empty_like(x)
    for c in range(n_cores):
        out[c * per:(c + 1) * per] = res.results[c]["yT"].transpose(0, 2, 1)
    return out
```

```python
import contextlib
import math
import numpy as np
import concourse.bass as bass
import concourse.mybir as mybir
from concourse.bass_utils import run_bass_kernel_spmd
from concourse.alu_op_type import AluOpType as ALU

F32 = mybir.dt.float32
BF16 = mybir.dt.bfloat16
I32 = mybir.dt.int32
AF = mybir.ActivationFunctionType

D = 1024
S = 2048
NCH = 8
DFF = 2816
NFF = 22
NQB = 16
EPS = 1e-6
PI = math.pi
PI_S = 3.1415925
RING = 6

C_G = 0
C_CW = 64
C_FCW = 88
C_INVF = 220
C_SIGN = 221
C_PM = 222
C_MLO = 350
C_MHI = 478
C_TOT = 606


class Eng:
    def __init__(self, name, sem, is_pe=False):
        self.name = name
        self.sem = sem
        self.cnt = 0
        self.ops = []
        self.waited = {}
        self.is_pe = is_pe

    def _waits(self, waits):
        for (s, v) in waits:
            if self.is_pe and s is self.sem:
                continue
            k = id(s)
            if self.waited.get(k, 0) >= v:
                continue
            self.waited[k] = v
            self.ops.append(("w", s, v))

    def emit(self, fn, waits, inc=True):
        self._waits(waits)
        self.ops.append(("i", fn, inc))
        if inc:
            self.cnt += 1
            return (self.sem, self.cnt)
        return (self.sem, self.cnt + 1)

    def emit_dma(self, fn, waits, dsem):
        self._waits(waits)
        self.ops.append(("d", fn, dsem.sem))
        dsem.cnt += 16
        return (dsem.sem, dsem.cnt)

    def replay(self, e):
        for op in self.ops:
            if op[0] == "w":
                e.wait_ge(op[1], op[2])
            elif op[0] == "i":
                ins = op[1](e)
                if op[2]:
                    ins.then_inc(self.sem, 1)
            else:
                ins = op[1](e)
                ins.then_inc(op[2], 16)


class DSem:
    def __init__(self, sem):
        self.sem = sem
        self.cnt = 0


class Buf:
    __slots__ = ("w", "r", "excl")

    def __init__(self, excl=False):
        self.w = None
        self.r = {}
        self.excl = excl


def _deps(reads, writes, eng=None):
    ws = []
    for b in reads:
        if b.w is not None:
            ws.append(b.w)
        if b.excl:
            for t in b.r.values():
                if eng is None or t[0] is not eng.sem:
                    ws.append(t)
    for b in writes:
        ws.extend(b.r.values())
        if b.w is not None:
            ws.append(b.w)
    return ws


def _note(tok, reads, writes):
    for b in reads:
        k = id(tok[0])
        old = b.r.get(k)
        if old is None or old[1] < tok[1]:
            b.r[k] = tok
    for b in writes:
        b.w = tok
        b.r = {}


def op(eng, fn, reads=(), writes=(), inc=True):
    inc = True
    tok = eng.emit(fn, _deps(reads, writes, eng), inc)
    _note(tok, reads, writes)
    return tok


def dma(eng, fn, reads, writes, dsem):
    tok = eng.emit_dma(fn, _deps(reads, writes), dsem)
    _note(tok, reads, writes)
    return tok


class Pool_:
    def __init__(self, items):
        self.items = items
        self.i = 0

    def next(self):
        it = self.items[self.i % len(self.items)]
        self.i += 1
        return it


class Item:
    __slots__ = ("ap", "buf")

    def __init__(self, ap):
        self.ap = ap
        self.buf = Buf()


def f_mm(out, lhsT, rhs, start, stop):
    return lambda e: e.matmul(out, lhsT=lhsT, rhs=rhs, start=start, stop=stop)


def f_act(out, in_, func, scale=None, bias=None):
    kw = {}
    if scale is not None:
        kw["scale"] = scale
    if bias is not None:
        kw["bias"] = bias
    return lambda e: e.activation(out=out, in_=in_, func=func, **kw)


def f_tt(out, in0, in1, o):
    return lambda e: e.tensor_tensor(out=out, in0=in0, in1=in1, op=o)


def f_ts(out, in0, s1, s2, o0, o1=None):
    if o1 is None:
        return lambda e: e.tensor_scalar(out=out, in0=in0, scalar1=s1, scalar2=None, op0=o0)
    return lambda e: e.tensor_scalar(out=out, in0=in0, scalar1=s1, scalar2=s2, op0=o0, op1=o1)


def f_stt(out, in0, scalar, in1, o0, o1):
    return lambda e: e.scalar_tensor_tensor(out=out, in0=in0, scalar=scalar, in1=in1, op0=o0, op1=o1)


def f_copy(out, in_):
    return lambda e: e.tensor_copy(out=out, in_=in_)


def f_memset(ap, v):
    return lambda e: e.memset(ap, v)


def f_dma(out, in_):
    return lambda e: e.dma_start(out=out, in_=in_)


def build_program(n_seq=2, stop_after=None, dbg_cols=0):
    nc = bass.Bass("TRN2", target_bir_lowering=False)
    dr = {}

    def din(name, shape, dt=F32):
        dr[name] = nc.dram_tensor(name, list(shape), dt, kind="ExternalInput").ap()
        return dr[name]

    xT = din("xT", [n_seq, D, S])
    posd = din("pos", [128, S], I32)
    cstd = din("cst", [128, C_TOT])
    sinkd = din("sink", [1, 16])
    wqk = din("wqk", [12, 128, 1024])
    wv = din("wv", [2, 128, 1024])
    wo = din("wo", [8, 128, 1024])
    win = din("win", [24, 128, 1024])
    wout = din("wout", [8, 128, 1024])
    wgu = din("wgu", [2, 44, 128, 1024])
    wd = din("wd", [2, 8, 128, DFF])
    yT = nc.dram_tensor("yT", [n_seq, D, S], F32, kind="ExternalOutput").ap()
    dbg = None
    if dbg_cols:
        dbg = nc.dram_tensor("dbg", [128, dbg_cols], F32, kind="ExternalOutput").ap()

    SCRB = 112640
    with contextlib.ExitStack() as es:
        E = es.enter_context
        h = E(nc.sbuf_tensor("h", [128, NCH, S], F32))
        scr = E(nc.sbuf_tensor("scr", [128, SCRB // 4], F32))
        ring_t = [E(nc.sbuf_tensor(f"ring{i}", [128, 1024], BF16)) for i in range(RING)]
        cst = E(nc.sbuf_tensor("cst_sb", [128, C_TOT], F32))
        ones = E(nc.sbuf_tensor("ones", [128, 128], BF16))
        pm = E(nc.sbuf_tensor("pm", [128, 128], BF16))
        mlo4 = E(nc.sbuf_tensor("mlo4", [128, 512], BF16))
        mhi4 = E(nc.sbuf_tensor("mhi4", [128, 512], BF16))
        zo = E(nc.sbuf_tensor("zo", [128, 128], BF16))
        epst = E(nc.sbuf_tensor("epst", [128, 1], F32))
        sinks = E(nc.sbuf_tensor("sinks", [1, 16], F32))
        es16 = E(nc.sbuf_tensor("es16", [1, 16], F32))
        hsave = E(nc.sbuf_tensor("hsave", [128, 32], F32))
        sq_t = [E(nc.sbuf_tensor(f"sq{i}", [128, 512], BF16)) for i in range(4)]
        rs_t = [E(nc.sbuf_tensor(f"rs{i}", [128, 512], F32)) for i in range(2)]
        banks_t = [E(nc.psum_tensor(f"ps{i}", [128, 512], F32)) for i in range(8)]

        sem = lambda n: E(nc.semaphore(n))
        PE = Eng("pe", sem("s_pe"), is_pe=True)
        ACT = Eng("act", sem("s_act"))
        DVE = Eng("dve", sem("s_dve"))
        POOL = Eng("pool", sem("s_pool"))
        SP = Eng("sp", sem("s_sp"))
        ENGS = [PE, ACT, DVE, POOL, SP]
        all_dsems = []

        def new_dsem(n):
            d = DSem(sem(n))
            all_dsems.append(d)
            return d

        ring_items = []
        for i in range(RING):
            it = Item(ring_t[i])
            ring_items.append(it)
        ring_sems = [new_dsem(f"d_ring{i}") for i in range(RING)]
        ring_ctr = [0]

        def ring_load(src_ap, n):
            i = ring_ctr[0] % RING
            ring_ctr[0] += 1
            it = ring_items[i]
            dma(POOL, f_dma(it.ap[:, 0:n], src_ap), [], [it.buf], ring_sems[i])
            return it

        ld_sems = [[new_dsem(f"d_ld{c}_{hf}") for hf in range(2)] for c in range(NCH)]
        st_sems = [[new_dsem(f"d_st{c}_{hf}") for hf in range(2)] for c in range(NCH)]
        misc_sem = new_dsem("d_misc")
        dbg_sem = new_dsem("d_dbg")

        hB = [[Buf() for _ in range(4)] for _ in range(NCH)]
        cstB = Buf()
        hsaveB = Buf()
        constB = Buf()
        banks = [Item(banks_t[i]) for i in range(8)]
        for b_ in banks:
            b_.buf.excl = True
        poolA = Pool_(banks[0:6])
        poolB = Pool_(banks[6:8])
        sq_pool = Pool_([Item(t) for t in sq_t])
        rs_pool = Pool_([Item(t) for t in rs_t])

        def scr_f32(off, n):
            assert off % 4 == 0 and off + 4 * n <= SCRB, (off, n)
            return scr[:, off // 4: off // 4 + n]

        def scr_bf(off, n):
            assert off % 4 == 0 and n % 2 == 0 and off + 2 * n <= SCRB, (off, n)
            return scr[:, off // 4: off // 4 + n // 2].bitcast(BF16)

        def scr_i32(off, n):
            assert off % 4 == 0 and off + 4 * n <= SCRB
            return scr[:, off // 4: off // 4 + n].bitcast(I32)

        def barrier():
            toks = []
            for e_ in ENGS[:3]:
                if e_.cnt > 0:
                    toks.append((e_.sem, e_.cnt))
            for d in all_dsems:
                if d.cnt > 0:
                    toks.append((d.sem, d.cnt))
            for e_ in ENGS:
                e_._waits(toks)

        def gcol(l, j, c):
            return cst[:, C_G + (l * 4 + j) * 8 + c: C_G + (l * 4 + j) * 8 + c + 1]

        def norm_cols(src, srcB, gl, gj, dst, dstB, n):
            ssb = poolB.next()
            for c in range(NCH):
                sq = sq_pool.next()
                op(ACT, f_act(sq.ap[:, 0:n], src[c], AF.Square), [srcB[c]], [sq.buf])
                op(PE, f_mm(ssb.ap[:, 0:n], ones[:, :], sq.ap[:, 0:n], c == 0, c == NCH - 1),
                   [sq.buf, constB], [ssb.buf], inc=(c == NCH - 1))
            rs = rs_pool.next()
            op(ACT, f_act(rs.ap[:, 0:n], ssb.ap[:, 0:n], AF.Ln, scale=1.0 / D, bias=epst[:, 0:1]),
               [ssb.buf, constB], [rs.buf])
            op(ACT, f_act(rs.ap[:, 0:n], rs.ap[:, 0:n], AF.Exp, scale=-0.5), [rs.buf], [rs.buf])
            for c in range(NCH):
                op(DVE, f_stt(dst[c], src[c], gcol(gl, gj, c), rs.ap[:, 0:n], ALU.mult, ALU.mult),
                   [srcB[c], rs.buf, cstB], [dstB[c]])

        def proj_postnorm(units_fn, nk, mov, movB, xbuf, l, gj, st):
            s0 = st * 1024
            xv = [[xbuf[:, (m * 1024 + sub * 512):(m * 1024 + sub * 512 + 512)] for sub in range(2)]
                  for m in range(NCH)]
            xB = [[Buf() for _ in range(2)] for _ in range(NCH)]
            ssb = [banks[6], banks[7]]
            pend = []

            def flush(item):
                m, sqs = item
                for sub in range(2):
                    op(PE, f_mm(ssb[sub].ap[:, :], ones[:, :], sqs[sub].ap[:, :], m == 0, m == NCH - 1),
                       [sqs[sub].buf, constB], [ssb[sub].buf], inc=True)

            for m in range(NCH):
                slots = units_fn(m)
                fb = [poolA.next(), poolA.next()]
                k = 0
                for (slot, nkk) in slots:
                    for kk in range(nkk):
                        for sub in range(2):
                            op(PE, f_mm(fb[sub].ap[:, :], slot.ap[:, kk * 128:(kk + 1) * 128],
                                        mov[k][:, sub * 512:(sub + 1) * 512], k == 0, k == nk - 1),
                               [slot.buf, movB[k]], [fb[sub].buf], inc=(k == nk - 1))
                        k += 1
                sqs = []
                for sub in range(2):
                    sq = sq_pool.next()
                    op(ACT, f_act(sq.ap[:, :], fb[sub].ap[:, :], AF.Square), [fb[sub].buf], [sq.buf])
                    op(ACT, f_act(xv[m][sub], fb[sub].ap[:, :], AF.Copy, scale=gcol(l, gj, m)),
                       [fb[sub].buf, cstB], [xB[m][sub]])
                    sqs.append(sq)
                pend.append((m, sqs))
                if len(pend) > 1:
                    flush(pend.pop(0))
            while pend:
                flush(pend.pop(0))
            for sub in range(2):
                rs = rs_pool.next()
                op(ACT, f_act(rs.ap[:, :], ssb[sub].ap[:, :], AF.Ln, scale=1.0 / D, bias=epst[:, 0:1]),
                   [ssb[sub].buf, constB], [rs.buf])
                op(ACT, f_act(rs.ap[:, :], rs.ap[:, :], AF.Exp, scale=-0.5), [rs.buf], [rs.buf])
                tt = 2 * st + sub
                for m in range(NCH):
                    hv = h[:, m, s0 + sub * 512: s0 + sub * 512 + 512]
                    op(DVE, f_tt(xv[m][sub], xv[m][sub], rs.ap[:, :], ALU.mult), [xB[m][sub], rs.buf], [xB[m][sub]])
                    op(DVE, f_tt(hv, hv, xv[m][sub], ALU.add), [xB[m][sub], hB[m][tt]], [hB[m][tt]])

        def norm_super(l, gj, st, hn, hnB):
            s0 = st * 1024
            for sub in range(2):
                tt = 2 * st + sub
                src = [h[:, c, s0 + sub * 512: s0 + sub * 512 + 512] for c in range(NCH)]
                dst = [hn[c][:, 1 + sub * 512: 1 + sub * 512 + 512] for c in range(NCH)]
                norm_cols(src, [hB[c][tt] for c in range(NCH)], l, gj, dst, hnB, 512)
            if st == 0:
                tok, hc, tt = 1024, 1025, 2
                src = [h[:, c, tok:tok + 1] for c in range(NCH)]
                dst = [hn[c][:, hc:hc + 1] for c in range(NCH)]
                norm_cols(src, [hB[c][tt] for c in range(NCH)], l, gj, dst, hnB, 1)
            else:
                hc = 0
            return hc

        halo_slots = [Item(banks[6].ap[:, i:i + 1]) for i in range(8)]
        for hs in halo_slots:
            hs.buf = banks[6].buf
        halo_ctr = [0]

        def proj_halo(slot, hn, hnB, hc, with_halo=True):
            b0, b1 = poolA.next(), poolA.next()
            hi = None
            if with_halo:
                hi = halo_slots[halo_ctr[0] % 8]
                halo_ctr[0] += 1
            for k in range(NCH):
                w = slot.ap[:, k * 128:(k + 1) * 128]
                op(PE, f_mm(b0.ap[:, :], w, hn[k][:, 1:513], k == 0, k == 7), [slot.buf, hnB[k]], [b0.buf],
                   inc=(k == 7))
                op(PE, f_mm(b1.ap[:, :], w, hn[k][:, 513:1025], k == 0, k == 7), [slot.buf, hnB[k]], [b1.buf],
                   inc=(k == 7))
                if with_halo:
                    op(PE, f_mm(hi.ap, w, hn[k][:, hc:hc + 1], k == 0, k == 7), [slot.buf, hnB[k]], [hi.buf],
                       inc=(k == 7))
            return b0, b1, hi

        def conv3(dst, src, wcol, srcB, dstB):
            op(DVE, f_ts(dst, src[:, 1:1025], wcol(1), None, ALU.mult), [srcB, cstB], [dstB])
            op(DVE, f_stt(dst, src[:, 0:1024], wcol(0), dst, ALU.mult, ALU.add), [srcB, dstB, cstB], [dstB])
            op(DVE, f_stt(dst, src[:, 2:1026], wcol(2), dst, ALU.mult, ALU.add), [srcB, dstB, cstB], [dstB])

        def ffn_phase(l):
            for st in range(2):
                barrier()
                hn = [scr_bf(c * 2052, 1026) for c in range(NCH)]
                hnB = [Buf() for _ in range(NCH)]
                xbuf = scr_f32(0, 8192)
                A0 = 32832
                actv = [scr_bf(A0 + j * 2048, 1024) for j in range(NFF)]
                actB = [Buf() for _ in range(NFF)]
                G0 = A0 + NFF * 2048
                gs_pool = Pool_([Item(scr_f32(G0 + i * 4112, 1026)) for i in range(2)])
                G1 = G0 + 2 * 4112
                gc_pool = Pool_([Item(scr_f32(G1 + i * 4096, 1024)) for i in range(2)])
                G2 = G1 + 2 * 4096
                sl_pool = Pool_([Item(scr_bf(G2 + i * 2048, 1024)) for i in range(2)])
                hc = norm_super(l, 2, st, hn, hnB)
                zc = 1025 - hc
                for j in range(NFF):
                    sg = ring_load(wgu[l, 2 * j], 1024)
                    su = ring_load(wgu[l, 2 * j + 1], 1024)
                    g0, g1, gh = proj_halo(sg, hn, hnB, hc, with_halo=(st == 0))
                    u0, u1, _ = proj_halo(su, hn, hnB, hc, with_halo=False)
                    gs = gs_pool.next()
                    op(ACT, f_act(gs.ap[:, 1:513], g0.ap[:, :], AF.Copy), [g0.buf], [gs.buf])
                    op(ACT, f_act(gs.ap[:, 513:1025], g1.ap[:, :], AF.Copy), [g1.buf], [gs.buf])
                    if st == 0:
                        op(ACT, f_act(gs.ap[:, hc:hc + 1], gh.ap, AF.Copy), [gh.buf], [gs.buf])
                        op(ACT, f_act(hsave[:, j:j + 1], g1.ap[:, 511:512], AF.Copy), [g1.buf], [hsaveB])
                    else:
                        op(ACT, f_act(gs.ap[:, 0:1], hsave[:, j:j + 1], AF.Copy), [hsaveB], [gs.buf])
                    op(DVE, f_memset(gs.ap[:, zc:zc + 1], 0.0), [], [gs.buf])
                    gc = gc_pool.next()
                    conv3(gc.ap, gs.ap, lambda jj, j=j: cst[:, C_FCW + (l * 3 + jj) * 22 + j: C_FCW + (l * 3 + jj) * 22 + j + 1],
                          gs.buf, gc.buf)
                    sl = sl_pool.next()
                    op(ACT, f_act(sl.ap[:, :], gc.ap[:, :], AF.Silu), [gc.buf], [sl.buf])
                    op(DVE, f_tt(actv[j][:, 0:512], sl.ap[:, 0:512], u0.ap[:, :], ALU.mult), [sl.buf, u0.buf], [actB[j]])
                    op(DVE, f_tt(actv[j][:, 512:1024], sl.ap[:, 512:1024], u1.ap[:, :], ALU.mult), [sl.buf, u1.buf], [actB[j]])

                def units_fn(m):
                    r = []
                    for (k0, nkk) in ((0, 8), (8, 8), (16, 6)):
                        r.append((ring_load(wd[l, m, :, k0 * 128:(k0 + nkk) * 128], nkk * 128), nkk))
                    return r
                proj_postnorm(units_fn, NFF, actv, actB, xbuf, l, 3, st)

        def attn_layer(seq):
            l = 0
            Q0, K0, V0, O0 = 0, 32768, 49152, 65536
            qrot = [scr_bf(Q0 + c * 4096, 2048) for c in range(8)]
            krot = [scr_bf(K0 + n * 4096, 2048) for n in range(4)]
            vaug_all = scr_bf(V0, 8192)
            qB = [[Buf() for _ in range(4)] for _ in range(8)]
            kB = [[Buf() for _ in range(4)] for _ in range(4)]
            vB = [Buf() for _ in range(16)]

            def vaug(blk, n):
                o = (blk * 4 + n) * 128
                return vaug_all[:, o:o + 128]

            barrier()
            if stop_after == 'startup':
                return {}
            X = 65536
            cosT = scr_f32(X, 2048)
            sinT = scr_f32(X + 8192, 2048)
            hn_ap = [scr_bf(X + 16384 + c * 1024, 512) for c in range(8)]
            U0 = X + 24576
            posi = scr_i32(U0, 2048)
            R0 = U0 + 8192
            ang = Item(scr_f32(R0, 512))
            ang2 = Item(scr_f32(R0 + 2048, 512))
            uu = Item(scr_f32(R0 + 4096, 512))
            ki = Item(scr_i32(R0 + 6144, 512))
            qb_pool = Pool_([Item(scr_bf(U0 + i * 1024, 512)) for i in range(2)])
            t1_pool = Pool_([Item(scr_f32(U0 + 2048 + i * 2048, 512)) for i in range(2)])
            t2_pool = Pool_([Item(scr_f32(U0 + 6144 + i * 2048, 512)) for i in range(2)])
            tabB = Buf()
            posB = Buf()
            dma(SP, f_dma(posi, posd[:, :]), [], [posB], misc_sem)
            v3 = vaug_all.rearrange("p (b c) -> p b c", c=128)
            op(DVE, f_memset(v3[:, :, 64:128], 1.0), [], vB)
            invf = cst[:, C_INVF:C_INVF + 1]
            sgn = cst[:, C_SIGN:C_SIGN + 1]
            for pc in range(4):
                cs = slice(pc * 512, (pc + 1) * 512)
                op(DVE, f_ts(ang.ap, posi[:, cs], invf, None, ALU.mult), [posB, cstB], [ang.buf])
                for (off, dstT, scl) in ((0.0, sinT, sgn), (PI / 2, cosT, None)):
                    op(DVE, f_ts(ang2.ap, ang.ap, off, None, ALU.add), [ang.buf], [ang2.buf])
                    op(DVE, f_ts(uu.ap, ang2.ap, 1.0 / (2 * PI), None, ALU.mult), [ang2.buf], [uu.buf])
                    op(DVE, f_copy(ki.ap, uu.ap), [uu.buf], [ki.buf])
                    op(DVE, f_copy(uu.ap, ki.ap), [ki.buf], [uu.buf])
                    op(DVE, f_stt(ang2.ap, uu.ap, -2 * PI, ang2.ap, ALU.mult, ALU.add), [uu.buf, ang2.buf], [ang2.buf])
                    op(DVE, f_ts(uu.ap, ang2.ap, PI, None, ALU.is_gt), [ang2.buf], [uu.buf])
                    op(DVE, f_stt(ang2.ap, uu.ap, -2 * PI, ang2.ap, ALU.mult, ALU.add), [uu.buf, ang2.buf], [ang2.buf])
                    op(DVE, f_ts(uu.ap, ang2.ap, -PI, None, ALU.is_lt), [ang2.buf], [uu.buf])
                    op(DVE, f_stt(ang2.ap, uu.ap, 2 * PI, ang2.ap, ALU.mult, ALU.add), [uu.buf, ang2.buf], [ang2.buf])
                    op(DVE, f_ts(ang2.ap, ang2.ap, -PI_S, PI_S, ALU.max, ALU.min), [ang2.buf], [ang2.buf])
                    if scl is None:
                        op(ACT, f_act(dstT[:, cs], ang2.ap, AF.Sin), [ang2.buf], [tabB])
                    else:
                        op(ACT, f_act(dstT[:, cs], ang2.ap, AF.Sin, scale=scl), [ang2.buf, cstB], [tabB])

            barrier()
            if stop_after == 'rope':
                return dict(cos=cosT, sin=sinT)
            hnB = [Buf() for _ in range(8)]
            for tt in range(4):
                ts_ = slice(tt * 512, (tt + 1) * 512)
                src = [h[:, c, ts_] for c in range(8)]
                norm_cols(src, [hB[c][tt] for c in range(8)], 0, 0, hn_ap, hnB, 512)
                if stop_after == 'norm':
                    return dict(hn=hn_ap)
                for m in range(12):
                    slot = ring_load(wqk[m], 1024)
                    bk = poolA.next()
                    for k in range(8):
                        op(PE, f_mm(bk.ap[:, :], slot.ap[:, k * 128:(k + 1) * 128], hn_ap[k], k == 0, k == 7),
                           [slot.buf, hnB[k]], [bk.buf], inc=(k == 7))
                    qb = qb_pool.next()
                    op(ACT, f_act(qb.ap, bk.ap[:, :], AF.Copy), [bk.buf], [qb.buf])
                    t1 = t1_pool.next()
                    op(DVE, f_tt(t1.ap, bk.ap[:, :], cosT[:, ts_], ALU.mult), [bk.buf, tabB], [t1.buf])
                    sw = poolB.next()
                    op(PE, f_mm(sw.ap[:, :], pm[:, :], qb.ap, True, True), [qb.buf, constB], [sw.buf])
                    t2 = t2_pool.next()
                    op(DVE, f_tt(t2.ap, sw.ap[:, :], sinT[:, ts_], ALU.mult), [sw.buf, tabB], [t2.buf])
                    if m < 8:
                        dst, dB = qrot[m][:, ts_], qB[m][tt]
                    else:
                        dst, dB = krot[m - 8][:, ts_], kB[m - 8][tt]
                    op(DVE, f_tt(dst, t1.ap, t2.ap, ALU.add), [t1.buf, t2.buf], [dB])
                if stop_after == 'qk':
                    return dict(q0=[qrot[c][:, 0:512] for c in range(8)])
                v0 = ring_load(wv[0], 1024)
                v1 = ring_load(wv[1], 1024)
                for b4 in range(4):
                    blk = tt * 4 + b4
                    bk = poolA.next()
                    for k in range(8):
                        vs = v0 if k < 4 else v1
                        op(PE, f_mm(bk.ap[:, 0:256], hn_ap[k][:, b4 * 128:(b4 + 1) * 128],
                                    vs.ap[:, (k % 4) * 256:(k % 4) * 256 + 256], k == 0, k == 7),
                           [vs.buf, hnB[k]], [bk.buf], inc=(k == 7))
                    dstv = v3[:, blk * 4:(blk + 1) * 4, 0:64]
                    srcv = bk.ap[:, 0:256].rearrange("p (n d) -> p n d", d=64)
                    op(ACT, f_act(dstv, srcv, AF.Copy), [bk.buf], [vB[blk]])
            if stop_after == "qkv":
                return dict(q=qrot, k=krot, v=vaug_all, cos=cosT, sin=sinT)

            barrier()
            ohat = [scr_bf(O0 + c * 4096, 2048) for c in range(8)]
            oB = [[Buf() for _ in range(16)] for _ in range(8)]
            Y = 98304
            pt_pool = Pool_([Item(scr_bf(Y + i * 3072, 1536)) for i in range(2)])
            lr_pool = Pool_([Item(scr_f32(Y + 6144 + i * 2048, 512)) for i in range(2)])
            es_row = scr_bf(Y + 10240, 2048)
            esB = Buf()
            op(DVE, f_memset(es_row[:, :], 0.0), [], [esB])
            op(ACT, f_act(es16[0:1, :], sinks[0:1, :], AF.Exp), [constB], [esB])
            for hh in range(16):
                op(DVE, f_ts(es_row[0:1, hh * 128:(hh + 1) * 128], ones[0:1, 0:128], es16[0:1, hh:hh + 1], None, ALU.mult),
                   [esB, constB], [esB])
            ohat3 = scr_bf(O0, 16384).rearrange("p (c t) -> p c t", c=8)
            def stage_a(i, n):
                kbs = [kb for kb in (i - 1, i, i + 1) if 0 <= kb < NQB]
                ptall = pt_pool.next()
                pt5 = ptall.ap.rearrange("p (k a b q) -> p k a b q", k=3, a=2, b=2)
                bank_of = {}
                for b in range(2):
                    bank_of[(b, 0)] = poolA.next()
                    if len(kbs) == 3:
                        bank_of[(b, 1)] = poolA.next()
                for idx, kb in enumerate(kbs):
                    for g in range(4):
                        hd = 4 * n + g
                        c = hd // 2
                        b = hd % 2
                        lo = b * 64
                        bk = bank_of[(b, idx // 2)]
                        col = ((idx % 2) * 2 + g // 2) * 128
                        op(PE, f_mm(bk.ap[:, col:col + 128],
                                    krot[n][lo:lo + 64, kb * 128:(kb + 1) * 128],
                                    qrot[c][lo:lo + 64, i * 128:(i + 1) * 128], True, True),
                           [kB[n][kb // 4], qB[c][i // 4]], [bk.buf])
                for half in range(2):
                    nk = min(2, len(kbs) - 2 * half)
                    if nk <= 0:
                        continue
                    for b in range(2):
                        bk = bank_of[(b, half)]
                        src = bk.ap[:, 0:nk * 256].rearrange("p (k a q) -> p k a q", k=nk, a=2)
                        op(ACT, f_act(pt5[:, 2 * half:2 * half + nk, :, b, :], src, AF.Exp, scale=0.125),
                           [bk.buf], [ptall.buf])
                pts = []
                for idx, kb in enumerate(kbs):
                    pt_ap = ptall.ap[:, idx * 512:(idx + 1) * 512]
                    if kb == i - 1:
                        op(DVE, f_tt(pt_ap, pt_ap, mlo4[:, :], ALU.mult), [ptall.buf, constB], [ptall.buf])
                    elif kb == i + 1:
                        op(DVE, f_tt(pt_ap, pt_ap, mhi4[:, :], ALU.mult), [ptall.buf, constB], [ptall.buf])
                    pts.append((kb, pt_ap))
                return (i, n, pts, ptall)

            def stage_b(state):
                i, n, pts, ptall = state
                od = poolB.next()
                for idx, (kb, pt_ap) in enumerate(pts):
                    op(PE, f_mm(od.ap[:, :], vaug(kb, n), pt_ap, idx == 0, False), [vB[kb], ptall.buf], [od.buf])
                op(PE, f_mm(od.ap[:, :], zo[:, :], es_row[:, n * 512:(n + 1) * 512], False, True),
                   [esB, constB], [od.buf])
                lr = lr_pool.next()
                op(ACT, f_act(lr.ap[64:128, :], od.ap[64:128, :], AF.Ln), [od.buf], [lr.buf])
                op(ACT, f_act(lr.ap[64:128, :], lr.ap[64:128, :], AF.Exp, scale=-1.0), [lr.buf], [lr.buf])
                o4 = od.ap[0:64, :].rearrange("p (a b q) -> p a b q", a=2, b=2)
                r4 = lr.ap[64:128, :].rearrange("p (a b q) -> p a b q", a=2, b=2)
                for b in range(2):
                    dsto = ohat3[b * 64:(b + 1) * 64, 2 * n:2 * n + 2, i * 128:(i + 1) * 128]
                    op(DVE, f_tt(dsto, o4[:, :, b, :], r4[:, :, b, :], ALU.mult), [od.buf, lr.buf],
                       [oB[2 * n][i], oB[2 * n + 1][i]])

            items = [(i, n) for i in range(NQB) for n in range(4)]
            prev = stage_a(*items[0])
            for it in items[1:]:
                cur = stage_a(*it)
                stage_b(prev)
                prev = cur
            stage_b(prev)
            if stop_after == "attn":
                return dict(o=ohat)

            barrier()
            xbuf = scr_f32(0, 8192)
            for st in range(2):
                mov = [ohat[k][:, st * 1024:(st + 1) * 1024] for k in range(8)]
                movB = []
                for k in range(8):
                    bb = Buf()
                    toks = [oB[k][i].w for i in range(st * 8, st * 8 + 8)]
                    bb.w = max(toks, key=lambda t: t[1])
                    movB.append(bb)
                proj_postnorm(lambda m: [(ring_load(wo[m], 1024), 8)], 8, mov, movB, xbuf, 0, 1, st)
                if st == 0:
                    barrier()
            return None

        def conv_layer():
            l = 1
            for st in range(2):
                barrier()
                hn = [scr_bf(c * 2052, 1026) for c in range(NCH)]
                hnB = [Buf() for _ in range(NCH)]
                A0 = 32832
                yb = [scr_bf(A0 + m * 2048, 1024) for m in range(8)]
                ybB = [Buf() for _ in range(8)]
                T0 = A0 + 16384
                xs_pool = Pool_([Item(scr_f32(T0 + i * 4112, 1026)) for i in range(2)])
                cx_pool = Pool_([Item(scr_f32(T0 + 8224 + i * 4112, 1026)) for i in range(2)])
                y_pool = Pool_([Item(scr_f32(T0 + 16448 + i * 4096, 1024)) for i in range(2)])
                hc = norm_super(l, 0, st, hn, hnB)
                zc = 1025 - hc
                for m in range(8):
                    sx = ring_load(win[16 + m], 1024)
                    sc = ring_load(win[8 + m], 1024)
                    sb = ring_load(win[m], 1024)
                    x0, x1, xh = proj_halo(sx, hn, hnB, hc, with_halo=(st == 0))
                    xs = xs_pool.next()
                    op(ACT, f_act(xs.ap[:, 1:513], x0.ap[:, :], AF.Copy), [x0.buf], [xs.buf])
                    op(ACT, f_act(xs.ap[:, 513:1025], x1.ap[:, :], AF.Copy), [x1.buf], [xs.buf])
                    if st == 0:
                        op(ACT, f_act(xs.ap[:, hc:hc + 1], xh.ap, AF.Copy), [xh.buf], [xs.buf])
                    c0, c1, ch = proj_halo(sc, hn, hnB, hc, with_halo=(st == 0))
                    cx = cx_pool.next()
                    op(DVE, f_tt(cx.ap[:, 1:513], c0.ap[:, :], xs.ap[:, 1:513], ALU.mult), [c0.buf, xs.buf], [cx.buf])
                    op(DVE, f_tt(cx.ap[:, 513:1025], c1.ap[:, :], xs.ap[:, 513:1025], ALU.mult), [c1.buf, xs.buf], [cx.buf])
                    if st == 0:
                        op(DVE, f_tt(cx.ap[:, hc:hc + 1], ch.ap, xs.ap[:, hc:hc + 1], ALU.mult), [ch.buf, xs.buf], [cx.buf])
                        op(DVE, f_copy(hsave[:, 24 + m:25 + m], cx.ap[:, 1024:1025]), [cx.buf], [hsaveB])
                    else:
                        op(DVE, f_copy(cx.ap[:, 0:1], hsave[:, 24 + m:25 + m]), [hsaveB], [cx.buf])
                    op(DVE, f_memset(cx.ap[:, zc:zc + 1], 0.0), [], [cx.buf])
                    y = y_pool.next()
                    conv3(y.ap, cx.ap, lambda jj, m=m: cst[:, C_CW + jj * 8 + m: C_CW + jj * 8 + m + 1], cx.buf, y.buf)
                    b0, b1, _ = proj_halo(sb, hn, hnB, hc, with_halo=False)
                    op(DVE, f_tt(yb[m][:, 0:512], y.ap[:, 0:512], b0.ap[:, :], ALU.mult), [y.buf, b0.buf], [ybB[m]])
                    op(DVE, f_tt(yb[m][:, 512:1024], y.ap[:, 512:1024], b1.ap[:, :], ALU.mult), [y.buf, b1.buf], [ybB[m]])
                xbuf = scr_f32(0, 8192)
                proj_postnorm(lambda m: [(ring_load(wout[m], 1024), 8)], 8, yb, ybB, xbuf, 1, 1, st)

        dma(SP, f_dma(cst[:, :], cstd[:, :]), [], [cstB], misc_sem)
        dma(SP, f_dma(sinks[0:1, :], sinkd[:, :]), [], [constB], misc_sem)
        op(DVE, f_memset(ones[:, :], 1.0), [], [constB])
        op(DVE, f_memset(epst[:, :], EPS), [], [constB])
        op(DVE, f_memset(zo[:, :], 0.0), [], [constB])
        op(DVE, f_memset(zo[0:1, 64:128], 1.0), [constB], [constB])
        op(DVE, f_copy(pm[:, :], cst[:, C_PM:C_PM + 128]), [cstB], [constB])
        for g in range(4):
            op(DVE, f_copy(mlo4[:, g * 128:(g + 1) * 128], cst[:, C_MLO:C_MLO + 128]), [cstB], [constB])
            op(DVE, f_copy(mhi4[:, g * 128:(g + 1) * 128], cst[:, C_MHI:C_MHI + 128]), [cstB], [constB])

        dump = None
        for seq in range(n_seq):
            for c in range(NCH):
                for hf in range(2):
                    dma(SP, f_dma(h[:, c, hf * 1024:(hf + 1) * 1024], xT[seq, c * 128:(c + 1) * 128, hf * 1024:(hf + 1) * 1024]),
                        [], [hB[c][2 * hf], hB[c][2 * hf + 1]], ld_sems[c][hf])
            dump = attn_layer(seq)
            if dump is not None:
                pass
            elif stop_after == "l0mix":
                pass
            else:
                ffn_phase(0)
                if stop_after != "l0":
                    conv_layer()
                    if stop_after != "l1mix":
                        ffn_phase(1)
            for c in range(NCH):
                for hf in range(2):
                    dma(SP, f_dma(yT[seq, c * 128:(c + 1) * 128, hf * 1024:(hf + 1) * 1024], h[:, c, hf * 1024:(hf + 1) * 1024]),
                        [hB[c][2 * hf], hB[c][2 * hf + 1]], [], st_sems[c][hf])

        if dump and dbg is not None:
            barrier()
            col = 0
            for name, aps in dump.items():
                aps = aps if isinstance(aps, list) else [aps]
                for a0 in aps:
                    for c0 in range(0, a0.shape[-1], 2048):
                        a = a0[:, c0:min(c0 + 2048, a0.shape[-1])]
                        n = a.shape[-1]
                        stg = scr_f32(SCRB - 8192, 2048)[:, 0:n]
                        bb = Buf()
                        op(DVE, f_copy(stg, a), [], [bb])
                        dma(SP, f_dma(dbg[:, col:col + n], stg), [bb], [], dbg_sem)
                        barrier()
                        col += n
        barrier()

        with nc.Block() as block:
            @block.tensor
            def _(e):
                PE.replay(e)

            @block.scalar
            def _(e):
                ACT.replay(e)

            @block.vector
            def _(e):
                DVE.replay(e)

            @block.gpsimd
            def _(e):
                POOL.replay(e)

            @block.sync
            def _(e):
                SP.replay(e)
    return nc


def _units(W, cols_list):
    K = W.shape[0]
    nk = K // 128
    Wr = W.reshape(nk, 128, W.shape[1])
    out = np.empty((len(cols_list), 128, nk * 128), np.float32)
    for i, ci in enumerate(cols_list):
        out[i] = Wr[:, :, ci].transpose(1, 0, 2).reshape(128, nk * 128)
    return out


def prep_shared(inp):
    f = lambda a: np.asarray(a, dtype=np.float32)
    wqkv = f(inp["attn_w_qkv"])[0]
    ar = np.arange(128)
    cols = [m * 128 + ar for m in range(8)] + [1024 + n * 64 + (ar % 64) for n in range(4)]
    sh = {}
    sh["wqk"] = _units(wqkv, cols)
    Wr = wqkv.reshape(8, 128, 1536)
    sh["wv"] = np.stack([Wr[4 * u:4 * u + 4, :, 1280:1536].transpose(1, 0, 2).reshape(128, 1024) for u in range(2)])
    sh["wo"] = _units(f(inp["attn_w_o"])[0], [m * 128 + ar for m in range(8)])
    sh["win"] = _units(f(inp["conv_w_in"])[0], [m * 128 + ar for m in range(24)])
    sh["wout"] = _units(f(inp["conv_w_out"])[0], [m * 128 + ar for m in range(8)])
    gu = f(inp["ffn_w_gate_up"])
    cols = []
    for j in range(NFF):
        cols.append(j * 128 + ar)
        cols.append(DFF + j * 128 + ar)
    sh["wgu"] = np.stack([_units(gu[l], cols) for l in range(2)])
    wdn = f(inp["ffn_w_down"])
    sh["wd"] = np.stack([_units(wdn[l], [m * 128 + ar for m in range(8)]) for l in range(2)])
    cst = np.zeros((128, C_TOT), np.float32)
    ng = f(inp["norm_gains"])
    for l in range(2):
        for j in range(4):
            cst[:, C_G + (l * 4 + j) * 8: C_G + (l * 4 + j) * 8 + 8] = ng[l, j].reshape(8, 128).T
    cw = f(inp["conv_w"])[0]
    for j in range(3):
        cst[:, C_CW + j * 8: C_CW + j * 8 + 8] = cw[j].reshape(8, 128).T
    fcw = f(inp["ffn_conv_w"])
    for l in range(2):
        for j in range(3):
            cst[:, C_FCW + (l * 3 + j) * 22: C_FCW + (l * 3 + j) * 22 + 22] = fcw[l, j].reshape(22, 128).T
    inv_freq = (10000.0 ** (-np.arange(0, 64, 2, dtype=np.float32) / np.float32(64))).astype(np.float32)
    p = np.arange(128)
    d = p % 64
    cst[:, C_INVF] = inv_freq[d % 32]
    cst[:, C_SIGN] = np.where(d < 32, -1.0, 1.0)
    partner = np.where(d < 32, p + 32, p - 32)
    cst[partner, C_PM + p] = 1.0
    r = p[:, None]
    t = p[None, :]
    cst[:, C_MLO:C_MLO + 128] = (t <= r).astype(np.float32)
    cst[:, C_MHI:C_MHI + 128] = (r <= t).astype(np.float32)
    sh["cst"] = cst
    sh["sink"] = f(inp["attn_sink"]).reshape(1, 16)
    sh["pos"] = np.ascontiguousarray(np.broadcast_to(np.asarray(inp["positions"], dtype=np.int32)[None, :], (128, S)))
    return sh


_NC_CACHE = {}


def kernel(**inputs):
    x = np.asarray(inputs["x"], dtype=np.float32)
    sh = prep_shared(inputs)
    n_cores = 8
    per = x.shape[0] // n_cores
    if "nc" not in _NC_CACHE:
        _NC_CACHE["nc"] = build_program(n_seq=per)
    nc = _NC_CACHE["nc"]
    in_maps = []
    for c in range(n_cores):
        m = dict(sh)
        m["xT"] = np.ascontiguousarray(x[c * per:(c + 1) * per].transpose(0, 2, 1))
        in_maps.append(m)
    res = run_bass_kernel_spmd(nc, in_maps, core_ids=list(range(n_cores)))
    out = np.empty_like(x)
    for c in range(n_cores):
        out[c * per:(c + 1) * per] = res.results[c]["yT"].transpose(0, 2, 1)
    return out
```

```python
import contextlib
import math
import numpy as np
import concourse.bass as bass
import concourse.mybir as mybir
from concourse.bass_utils import run_bass_kernel_spmd
from concourse.alu_op_type import AluOpType as ALU

F32 = mybir.dt.float32
BF16 = mybir.dt.bfloat16
I32 = mybir.dt.int32
AF = mybir.ActivationFunctionType

D = 1024
S = 2048
NCH = 8
DFF = 2816
NFF = 22
NQB = 16
EPS = 1e-6
PI = math.pi
PI_S = 3.1415925
RING = 6

C_G = 0
C_CW = 64
C_FCW = 88
C_INVF = 220
C_SIGN = 221
C_PM = 222
C_MLO = 350
C_MHI = 478
C_TOT = 606


class Eng:
    def __init__(self, name, sem, is_pe=False):
        self.name = name
        self.sem = sem
        self.cnt = 0
        self.ops = []
        self.waited = {}
        self.is_pe = is_pe

    def _waits(self, waits):
        for (s, v) in waits:
            if self.is_pe and s is self.sem:
                continue
            k = id(s)
            if self.waited.get(k, 0) >= v:
                continue
            self.waited[k] = v
            self.ops.append(("w", s, v))

    def emit(self, fn, waits, inc=True):
        self._waits(waits)
        self.ops.append(("i", fn, inc))
        if inc:
            self.cnt += 1
            return (self.sem, self.cnt)
        return (self.sem, self.cnt + 1)

    def emit_dma(self, fn, waits, dsem):
        self._waits(waits)
        self.ops.append(("d", fn, dsem.sem))
        dsem.cnt += 16
        return (dsem.sem, dsem.cnt)

    def replay(self, e):
        for op in self.ops:
            if op[0] == "w":
                e.wait_ge(op[1], op[2])
            elif op[0] == "i":
                ins = op[1](e)
                if op[2]:
                    ins.then_inc(self.sem, 1)
            else:
                ins = op[1](e)
                ins.then_inc(op[2], 16)


class DSem:
    def __init__(self, sem):
        self.sem = sem
        self.cnt = 0


class Buf:
    __slots__ = ("w", "r", "excl")

    def __init__(self, excl=False):
        self.w = None
        self.r = {}
        self.excl = excl


def _deps(reads, writes, eng=None):
    ws = []
    for b in reads:
        if b.w is not None:
            ws.append(b.w)
        if b.excl:
            for t in b.r.values():
                if eng is None or t[0] is not eng.sem:
                    ws.append(t)
    for b in writes:
        ws.extend(b.r.values())
        if b.w is not None:
            ws.append(b.w)
    return ws


def _note(tok, reads, writes):
    for b in reads:
        k = id(tok[0])
        old = b.r.get(k)
        if old is None or old[1] < tok[1]:
            b.r[k] = tok
    for b in writes:
        b.w = tok
        b.r = {}


def op(eng, fn, reads=(), writes=(), inc=True):
    inc = True
    tok = eng.emit(fn, _deps(reads, writes, eng), inc)
    _note(tok, reads, writes)
    return tok


def dma(eng, fn, reads, writes, dsem):
    tok = eng.emit_dma(fn, _deps(reads, writes), dsem)
    _note(tok, reads, writes)
    return tok


class Pool_:
    def __init__(self, items):
        self.items = items
        self.i = 0

    def next(self):
        it = self.items[self.i % len(self.items)]
        self.i += 1
        return it


class Item:
    __slots__ = ("ap", "buf")

    def __init__(self, ap):
        self.ap = ap
        self.buf = Buf()


def f_mm(out, lhsT, rhs, start, stop):
    return lambda e: e.matmul(out, lhsT=lhsT, rhs=rhs, start=start, stop=stop)


def f_act(out, in_, func, scale=None, bias=None):
    kw = {}
    if scale is not None:
        kw["scale"] = scale
    if bias is not None:
        kw["bias"] = bias
    return lambda e: e.activation(out=out, in_=in_, func=func, **kw)


def f_tt(out, in0, in1, o):
    return lambda e: e.tensor_tensor(out=out, in0=in0, in1=in1, op=o)


def f_ts(out, in0, s1, s2, o0, o1=None):
    if o1 is None:
        return lambda e: e.tensor_scalar(out=out, in0=in0, scalar1=s1, scalar2=None, op0=o0)
    return lambda e: e.tensor_scalar(out=out, in0=in0, scalar1=s1, scalar2=s2, op0=o0, op1=o1)


def f_stt(out, in0, scalar, in1, o0, o1):
    return lambda e: e.scalar_tensor_tensor(out=out, in0=in0, scalar=scalar, in1=in1, op0=o0, op1=o1)


def f_copy(out, in_):
    return lambda e: e.tensor_copy(out=out, in_=in_)


def f_memset(ap, v):
    return lambda e: e.memset(ap, v)


def f_dma(out, in_):
    return lambda e: e.dma_start(out=out, in_=in_)


def build_program(n_seq=2, stop_after=None, dbg_cols=0):
    nc = bass.Bass("TRN2", target_bir_lowering=False)
    dr = {}

    def din(name, shape, dt=F32):
        dr[name] = nc.dram_tensor(name, list(shape), dt, kind="ExternalInput").ap()
        return dr[name]

    xT = din("xT", [n_seq, D, S])
    posd = din("pos", [128, S], I32)
    cstd = din("cst", [128, C_TOT])
    sinkd = din("sink", [1, 16])
    wqk = din("wqk", [12, 128, 1024])
    wv = din("wv", [2, 128, 1024])
    wo = din("wo", [8, 128, 1024])
    win = din("win", [24, 128, 1024])
    wout = din("wout", [8, 128, 1024])
    wgu = din("wgu", [2, 44, 128, 1024])
    wd = din("wd", [2, 8, 128, DFF])
    yT = nc.dram_tensor("yT", [n_seq, D, S], F32, kind="ExternalOutput").ap()
    dbg = None
    if dbg_cols:
        dbg = nc.dram_tensor("dbg", [128, dbg_cols], F32, kind="ExternalOutput").ap()

    SCRB = 114784
    with contextlib.ExitStack() as es:
        E = es.enter_context
        h = E(nc.sbuf_tensor("h", [128, NCH, S], F32))
        scr = E(nc.sbuf_tensor("scr", [128, SCRB // 4], F32))
        ring_t = [E(nc.sbuf_tensor(f"ring{i}", [128, 1024], BF16)) for i in range(RING)]
        cst = E(nc.sbuf_tensor("cst_sb", [128, C_TOT], F32))
        ones = E(nc.sbuf_tensor("ones", [128, 128], BF16))
        pm = E(nc.sbuf_tensor("pm", [128, 128], BF16))
        mlo4 = E(nc.sbuf_tensor("mlo4", [128, 512], BF16))
        mhi4 = E(nc.sbuf_tensor("mhi4", [128, 512], BF16))
        zo = E(nc.sbuf_tensor("zo", [128, 128], BF16))
        epst = E(nc.sbuf_tensor("epst", [128, 1], F32))
        sinks = E(nc.sbuf_tensor("sinks", [1, 16], F32))
        es16 = E(nc.sbuf_tensor("es16", [1, 16], F32))
        hsave = E(nc.sbuf_tensor("hsave", [128, 32], F32))
        sq_t = [E(nc.sbuf_tensor(f"sq{i}", [128, 512], BF16)) for i in range(4)]
        rs_t = [E(nc.sbuf_tensor(f"rs{i}", [128, 512], F32)) for i in range(2)]
        banks_t = [E(nc.psum_tensor(f"ps{i}", [128, 512], F32)) for i in range(8)]

        sem = lambda n: E(nc.semaphore(n))
        PE = Eng("pe", sem("s_pe"), is_pe=True)
        ACT = Eng("act", sem("s_act"))
        DVE = Eng("dve", sem("s_dve"))
        POOL = Eng("pool", sem("s_pool"))
        SP = Eng("sp", sem("s_sp"))
        ENGS = [PE, ACT, DVE, POOL, SP]
        all_dsems = []

        def new_dsem(n):
            d = DSem(sem(n))
            all_dsems.append(d)
            return d

        ring_items = []
        for i in range(RING):
            it = Item(ring_t[i])
            ring_items.append(it)
        ring_sems = [new_dsem(f"d_ring{i}") for i in range(RING)]
        ring_ctr = [0]

        def ring_load(src_ap, n):
            i = ring_ctr[0] % RING
            ring_ctr[0] += 1
            it = ring_items[i]
            dma(POOL, f_dma(it.ap[:, 0:n], src_ap), [], [it.buf], ring_sems[i])
            return it

        ld_sems = [[new_dsem(f"d_ld{c}_{hf}") for hf in range(2)] for c in range(NCH)]
        st_sems = [[new_dsem(f"d_st{c}_{hf}") for hf in range(2)] for c in range(NCH)]
        misc_sem = new_dsem("d_misc")
        dbg_sem = new_dsem("d_dbg")

        hB = [[Buf() for _ in range(4)] for _ in range(NCH)]
        cstB = Buf()
        hsaveB = Buf()
        constB = Buf()
        banks = [Item(banks_t[i]) for i in range(8)]
        for b_ in banks:
            b_.buf.excl = True
        poolA = Pool_(banks[0:6])
        poolB = Pool_(banks[6:8])
        sq_pool = Pool_([Item(t) for t in sq_t])
        rs_pool = Pool_([Item(t) for t in rs_t])

        def scr_f32(off, n):
            assert off % 4 == 0 and off + 4 * n <= SCRB, (off, n)
            return scr[:, off // 4: off // 4 + n]

        def scr_bf(off, n):
            assert off % 4 == 0 and n % 2 == 0 and off + 2 * n <= SCRB, (off, n)
            return scr[:, off // 4: off // 4 + n // 2].bitcast(BF16)

        def scr_i32(off, n):
            assert off % 4 == 0 and off + 4 * n <= SCRB
            return scr[:, off // 4: off // 4 + n].bitcast(I32)

        deferred = []

        def drain(n=None):
            k = len(deferred) if n is None else min(n, len(deferred))
            for _ in range(k):
                deferred.pop(0)()

        def barrier(final=False):
            drain()
            toks = []
            for e_ in ENGS[:3]:
                if e_.cnt > 0:
                    toks.append((e_.sem, e_.cnt))
            for d in all_dsems:
                if d.cnt > 0:
                    toks.append((d.sem, d.cnt))
            for e_ in ENGS:
                if e_ is POOL and not final:
                    continue
                e_._waits(toks)

        def gcol(l, j, c):
            return cst[:, C_G + (l * 4 + j) * 8 + c: C_G + (l * 4 + j) * 8 + c + 1]

        def norm_cols(src, srcB, gl, gj, dst, dstB, n, ssp=None):
            drain()
            ssb = (ssp or poolB).next()
            for c in range(NCH):
                sq = sq_pool.next()
                op(ACT, f_act(sq.ap[:, 0:n], src[c], AF.Square), [srcB[c]], [sq.buf])
                op(PE, f_mm(ssb.ap[:, 0:n], ones[:, :], sq.ap[:, 0:n], c == 0, c == NCH - 1),
                   [sq.buf, constB], [ssb.buf], inc=(c == NCH - 1))
            rs = rs_pool.next()
            op(ACT, f_act(rs.ap[:, 0:n], ssb.ap[:, 0:n], AF.Ln, scale=1.0 / D, bias=epst[:, 0:1]),
               [ssb.buf, constB], [rs.buf])
            op(ACT, f_act(rs.ap[:, 0:n], rs.ap[:, 0:n], AF.Exp, scale=-0.5), [rs.buf], [rs.buf])
            for c in range(NCH):
                op(DVE, f_stt(dst[c], src[c], gcol(gl, gj, c), rs.ap[:, 0:n], ALU.mult, ALU.mult),
                   [srcB[c], rs.buf, cstB], [dstB[c]])

        def make_x(xbuf):
            xv = [[xbuf[:, (m * 1024 + sub * 512):(m * 1024 + sub * 512 + 512)] for sub in range(2)]
                  for m in range(NCH)]
            xB = [[Buf() for _ in range(2)] for _ in range(NCH)]
            return xv, xB

        def proj_postnorm(units_fn, nk, mov, movB, xvb, l, gj, st, mid_hook=None, defer=False):
            drain()
            s0 = st * 1024
            xv, xB = xvb
            ssb = [banks[6], banks[7]]
            pend = []

            def flush(item):
                m, sqs = item
                for sub in range(2):
                    op(PE, f_mm(ssb[sub].ap[:, :], ones[:, :], sqs[sub].ap[:, :], m == 0, m == NCH - 1),
                       [sqs[sub].buf, constB], [ssb[sub].buf], inc=True)

            for m in range(NCH):
                if m == 4 and mid_hook is not None:
                    while pend:
                        flush(pend.pop(0))
                    mid_hook()
                slots = units_fn(m)
                fb = [poolA.next(), poolA.next()]
                k = 0
                for (slot, nkk) in slots:
                    for kk in range(nkk):
                        for sub in range(2):
                            op(PE, f_mm(fb[sub].ap[:, :], slot.ap[:, kk * 128:(kk + 1) * 128],
                                        mov[k][:, sub * 512:(sub + 1) * 512], k == 0, k == nk - 1),
                               [slot.buf, movB[k]], [fb[sub].buf], inc=(k == nk - 1))
                        k += 1
                sqs = []
                for sub in range(2):
                    sq = sq_pool.next()
                    op(ACT, f_act(sq.ap[:, :], fb[sub].ap[:, :], AF.Square), [fb[sub].buf], [sq.buf])
                    op(ACT, f_act(xv[m][sub], fb[sub].ap[:, :], AF.Copy, scale=gcol(l, gj, m)),
                       [fb[sub].buf, cstB], [xB[m][sub]])
                    sqs.append(sq)
                pend.append((m, sqs))
                if len(pend) > 1:
                    flush(pend.pop(0))
            while pend:
                flush(pend.pop(0))

            def tail_pair(m, sub, rs):
                def go():
                    tt = 2 * st + sub
                    hv = h[:, m, s0 + sub * 512: s0 + sub * 512 + 512]
                    op(DVE, f_tt(xv[m][sub], xv[m][sub], rs.ap[:, :], ALU.mult), [xB[m][sub], rs.buf], [xB[m][sub]])
                    op(DVE, f_tt(hv, hv, xv[m][sub], ALU.add), [xB[m][sub], hB[m][tt]], [hB[m][tt]])
                return go

            for sub in range(2):
                rs = rs_pool.next()
                op(ACT, f_act(rs.ap[:, :], ssb[sub].ap[:, :], AF.Ln, scale=1.0 / D, bias=epst[:, 0:1]),
                   [ssb[sub].buf, constB], [rs.buf])
                op(ACT, f_act(rs.ap[:, :], rs.ap[:, :], AF.Exp, scale=-0.5), [rs.buf], [rs.buf])
                for m in range(NCH):
                    deferred.append(tail_pair(m, sub, rs))
            if not defer:
                drain()

        def norm_super(l, gj, st, hn, hnB, ssp=None):
            s0 = st * 1024
            for sub in range(2):
                tt = 2 * st + sub
                src = [h[:, c, s0 + sub * 512: s0 + sub * 512 + 512] for c in range(NCH)]
                dst = [hn[c][:, 1 + sub * 512: 1 + sub * 512 + 512] for c in range(NCH)]
                norm_cols(src, [hB[c][tt] for c in range(NCH)], l, gj, dst, hnB, 512, ssp)
            if st == 0:
                tok, hc, tt = 1024, 1025, 2
                src = [h[:, c, tok:tok + 1] for c in range(NCH)]
                dst = [hn[c][:, hc:hc + 1] for c in range(NCH)]
                norm_cols(src, [hB[c][tt] for c in range(NCH)], l, gj, dst, hnB, 1, ssp)
            else:
                hc = 0
            return hc

        halo_slots = [Item(banks[6].ap[:, i:i + 1]) for i in range(8)]
        for hs in halo_slots:
            hs.buf = banks[6].buf
        halo_ctr = [0]

        def proj_halo(slot, hn, hnB, hc, with_halo=True):
            b0, b1 = poolA.next(), poolA.next()
            hi = None
            if with_halo:
                hi = halo_slots[halo_ctr[0] % 8]
                halo_ctr[0] += 1
            for k in range(NCH):
                w = slot.ap[:, k * 128:(k + 1) * 128]
                op(PE, f_mm(b0.ap[:, :], w, hn[k][:, 1:513], k == 0, k == 7), [slot.buf, hnB[k]], [b0.buf],
                   inc=(k == 7))
                op(PE, f_mm(b1.ap[:, :], w, hn[k][:, 513:1025], k == 0, k == 7), [slot.buf, hnB[k]], [b1.buf],
                   inc=(k == 7))
                if with_halo:
                    op(PE, f_mm(hi.ap, w, hn[k][:, hc:hc + 1], k == 0, k == 7), [slot.buf, hnB[k]], [hi.buf],
                       inc=(k == 7))
            return b0, b1, hi

        def conv3(dst, src, wcol, srcB, dstB):
            op(DVE, f_ts(dst, src[:, 1:1025], wcol(1), None, ALU.mult), [srcB, cstB], [dstB])
            op(DVE, f_stt(dst, src[:, 0:1024], wcol(0), dst, ALU.mult, ALU.add), [srcB, dstB, cstB], [dstB])
            op(DVE, f_stt(dst, src[:, 2:1026], wcol(2), dst, ALU.mult, ALU.add), [srcB, dstB, cstB], [dstB])

        def ffn_phase(l, after_gu1=None):
            barrier()
            xvb = make_x(scr_f32(0, 8192))
            H0 = 32768
            hn = [scr_bf(H0 + c * 2056, 1026) for c in range(NCH)]
            hnB = [Buf() for _ in range(NCH)]
            A0 = H0 + 16448
            actv = [scr_bf(A0 + j * 2048, 1024) for j in range(NFF)]
            actB = [Buf() for _ in range(NFF)]
            G0 = A0 + NFF * 2048
            gs_pool = Pool_([Item(scr_f32(G0 + i * 4112, 1026)) for i in range(2)])
            G1 = G0 + 2 * 4112
            gc_pool = Pool_([Item(scr_f32(G1 + i * 4096, 1024)) for i in range(2)])
            G2 = G1 + 2 * 4096
            sl_pool = Pool_([Item(scr_bf(G2 + i * 2048, 1024)) for i in range(2)])

            def gu_loop(st, hc):
                zc = 1025 - hc
                for j in range(NFF):
                    sg = ring_load(wgu[l, 2 * j], 1024)
                    su = ring_load(wgu[l, 2 * j + 1], 1024)
                    g0, g1, gh = proj_halo(sg, hn, hnB, hc, with_halo=(st == 0))
                    u0, u1, _ = proj_halo(su, hn, hnB, hc, with_halo=False)
                    gs = gs_pool.next()
                    op(ACT, f_act(gs.ap[:, 1:513], g0.ap[:, :], AF.Copy), [g0.buf], [gs.buf])
                    op(ACT, f_act(gs.ap[:, 513:1025], g1.ap[:, :], AF.Copy), [g1.buf], [gs.buf])
                    if st == 0:
                        op(ACT, f_act(gs.ap[:, hc:hc + 1], gh.ap, AF.Copy), [gh.buf], [gs.buf])
                        op(ACT, f_act(hsave[:, j:j + 1], g1.ap[:, 511:512], AF.Copy), [g1.buf], [hsaveB])
                    else:
                        op(ACT, f_act(gs.ap[:, 0:1], hsave[:, j:j + 1], AF.Copy), [hsaveB], [gs.buf])
                    op(DVE, f_memset(gs.ap[:, zc:zc + 1], 0.0), [], [gs.buf])
                    gc = gc_pool.next()
                    conv3(gc.ap, gs.ap, lambda jj, j=j: cst[:, C_FCW + (l * 3 + jj) * 22 + j: C_FCW + (l * 3 + jj) * 22 + j + 1],
                          gs.buf, gc.buf)
                    sl = sl_pool.next()
                    op(ACT, f_act(sl.ap[:, :], gc.ap[:, :], AF.Silu), [gc.buf], [sl.buf])
                    op(DVE, f_tt(actv[j][:, 0:512], sl.ap[:, 0:512], u0.ap[:, :], ALU.mult), [sl.buf, u0.buf], [actB[j]])
                    op(DVE, f_tt(actv[j][:, 512:1024], sl.ap[:, 512:1024], u1.ap[:, :], ALU.mult), [sl.buf, u1.buf], [actB[j]])
                    drain(4)

            def units_fn(m):
                r = []
                for (k0, nkk) in ((0, 8), (8, 8), (16, 6)):
                    r.append((ring_load(wd[l, m, :, k0 * 128:(k0 + nkk) * 128], nkk * 128), nkk))
                return r

            hc = norm_super(l, 2, 0, hn, hnB)
            gu_loop(0, hc)
            hc1 = [0]
            proj_postnorm(units_fn, NFF, actv, actB, xvb, l, 3, 0,
                          mid_hook=lambda: hc1.__setitem__(0, norm_super(l, 2, 1, hn, hnB, ssp=poolA)), defer=True)
            gu_loop(1, hc1[0])
            if after_gu1 is not None:
                drain()
                after_gu1()
            proj_postnorm(units_fn, NFF, actv, actB, xvb, l, 3, 1)

        def attn_layer(seq):
            l = 0
            Q0, K0, V0, O0 = 0, 32768, 49152, 65536
            qrot = [scr_bf(Q0 + c * 4096, 2048) for c in range(8)]
            krot = [scr_bf(K0 + n * 4096, 2048) for n in range(4)]
            vaug_all = scr_bf(V0, 8192)
            qB = [[Buf() for _ in range(4)] for _ in range(8)]
            kB = [[Buf() for _ in range(4)] for _ in range(4)]
            vB = [Buf() for _ in range(16)]

            def vaug(blk, n):
                o = (blk * 4 + n) * 128
                return vaug_all[:, o:o + 128]

            barrier()
            if stop_after == 'startup':
                return {}
            X = 65536
            cosT = scr_f32(X, 2048)
            sinT = scr_f32(X + 8192, 2048)
            hn_ap = [scr_bf(X + 16384 + c * 1024, 512) for c in range(8)]
            U0 = X + 24576
            posi = scr_i32(U0, 2048)
            R0 = U0 + 8192
            ang = Item(scr_f32(R0, 512))
            ang2 = Item(scr_f32(R0 + 2048, 512))
            uu = Item(scr_f32(R0 + 4096, 512))
            ki = Item(scr_i32(R0 + 6144, 512))
            Q0_ = R0 + 8192
            qb_pool = Pool_([Item(scr_bf(Q0_ + i * 1024, 512)) for i in range(2)])
            t1_pool = Pool_([Item(scr_f32(Q0_ + 2048 + i * 2048, 512)) for i in range(2)])
            t2_pool = Pool_([Item(scr_f32(Q0_ + 6144 + i * 2048, 512)) for i in range(1)])
            tabBs = [Buf() for _ in range(4)]
            posB = Buf()
            dma(SP, f_dma(posi, posd[:, :]), [], [posB], misc_sem)
            v3 = vaug_all.rearrange("p (b c) -> p b c", c=128)
            op(DVE, f_memset(v3[:, :, 64:128], 1.0), [], vB)
            invf = cst[:, C_INVF:C_INVF + 1]
            sgn = cst[:, C_SIGN:C_SIGN + 1]
            def rope_piece(pc):
                tabB = tabBs[pc]
                cs = slice(pc * 512, (pc + 1) * 512)
                op(DVE, f_ts(ang.ap, posi[:, cs], invf, None, ALU.mult), [posB, cstB], [ang.buf])
                for (off, dstT, scl) in ((0.0, sinT, sgn), (PI / 2, cosT, None)):
                    op(DVE, f_ts(ang2.ap, ang.ap, off, None, ALU.add), [ang.buf], [ang2.buf])
                    op(DVE, f_ts(uu.ap, ang2.ap, 1.0 / (2 * PI), None, ALU.mult), [ang2.buf], [uu.buf])
                    op(DVE, f_copy(ki.ap, uu.ap), [uu.buf], [ki.buf])
                    op(DVE, f_copy(uu.ap, ki.ap), [ki.buf], [uu.buf])
                    op(DVE, f_stt(ang2.ap, uu.ap, -2 * PI, ang2.ap, ALU.mult, ALU.add), [uu.buf, ang2.buf], [ang2.buf])
                    op(DVE, f_ts(uu.ap, ang2.ap, PI, None, ALU.is_gt), [ang2.buf], [uu.buf])
                    op(DVE, f_stt(ang2.ap, uu.ap, -2 * PI, ang2.ap, ALU.mult, ALU.add), [uu.buf, ang2.buf], [ang2.buf])
                    op(DVE, f_ts(uu.ap, ang2.ap, -PI, None, ALU.is_lt), [ang2.buf], [uu.buf])
                    op(DVE, f_stt(ang2.ap, uu.ap, 2 * PI, ang2.ap, ALU.mult, ALU.add), [uu.buf, ang2.buf], [ang2.buf])
                    op(DVE, f_ts(ang2.ap, ang2.ap, -PI_S, PI_S, ALU.max, ALU.min), [ang2.buf], [ang2.buf])
                    if scl is None:
                        op(ACT, f_act(dstT[:, cs], ang2.ap, AF.Sin), [ang2.buf], [tabB])
                    else:
                        op(ACT, f_act(dstT[:, cs], ang2.ap, AF.Sin, scale=scl), [ang2.buf, cstB], [tabB])

            hnB = [Buf() for _ in range(8)]
            rope_piece(0)
            for tt in range(4):
                tabB = tabBs[tt]
                ts_ = slice(tt * 512, (tt + 1) * 512)
                src = [h[:, c, ts_] for c in range(8)]
                norm_cols(src, [hB[c][tt] for c in range(8)], 0, 0, hn_ap, hnB, 512)
                if tt < 3:
                    rope_piece(tt + 1)
                lagq = []

                def rope_finish(item):
                    m, qb, t1 = item
                    sw = poolB.next()
                    op(PE, f_mm(sw.ap[:, :], pm[:, :], qb.ap, True, True), [qb.buf, constB], [sw.buf])
                    t2 = t2_pool.next()
                    op(DVE, f_tt(t2.ap, sw.ap[:, :], sinT[:, ts_], ALU.mult), [sw.buf, tabB], [t2.buf])
                    if m < 8:
                        dst, dB = qrot[m][:, ts_], qB[m][tt]
                    else:
                        dst, dB = krot[m - 8][:, ts_], kB[m - 8][tt]
                    op(DVE, f_tt(dst, t1.ap, t2.ap, ALU.add), [t1.buf, t2.buf], [dB])

                for m in range(12):
                    slot = ring_load(wqk[m], 1024)
                    bk = poolA.next()
                    for k in range(8):
                        op(PE, f_mm(bk.ap[:, :], slot.ap[:, k * 128:(k + 1) * 128], hn_ap[k], k == 0, k == 7),
                           [slot.buf, hnB[k]], [bk.buf], inc=(k == 7))
                    if lagq:
                        rope_finish(lagq.pop(0))
                    qb = qb_pool.next()
                    op(ACT, f_act(qb.ap, bk.ap[:, :], AF.Copy), [bk.buf], [qb.buf])
                    t1 = t1_pool.next()
                    op(DVE, f_tt(t1.ap, bk.ap[:, :], cosT[:, ts_], ALU.mult), [bk.buf, tabB], [t1.buf])
                    lagq.append((m, qb, t1))
                while lagq:
                    rope_finish(lagq.pop(0))
                if stop_after == 'qk':
                    return dict(q0=[qrot[c][:, 0:512] for c in range(8)])
                v0 = ring_load(wv[0], 1024)
                v1 = ring_load(wv[1], 1024)
                for b4 in range(4):
                    blk = tt * 4 + b4
                    bk = poolA.next()
                    for k in range(8):
                        vs = v0 if k < 4 else v1
                        op(PE, f_mm(bk.ap[:, 0:256], hn_ap[k][:, b4 * 128:(b4 + 1) * 128],
                                    vs.ap[:, (k % 4) * 256:(k % 4) * 256 + 256], k == 0, k == 7),
                           [vs.buf, hnB[k]], [bk.buf], inc=(k == 7))
                    dstv = v3[:, blk * 4:(blk + 1) * 4, 0:64]
                    srcv = bk.ap[:, 0:256].rearrange("p (n d) -> p n d", d=64)
                    op(ACT, f_act(dstv, srcv, AF.Copy), [bk.buf], [vB[blk]])
            if stop_after == "qkv":
                return dict(q=qrot, k=krot, v=vaug_all, cos=cosT, sin=sinT)

            barrier()
            ohat = [scr_bf(O0 + c * 4096, 2048) for c in range(8)]
            oB = [[Buf() for _ in range(16)] for _ in range(8)]
            Y = 98304
            pt_pool = Pool_([Item(scr_bf(Y + i * 3072, 1536)) for i in range(2)])
            lr_pool = Pool_([Item(scr_f32(Y + 6144 + i * 2048, 512)) for i in range(2)])
            es_row = scr_bf(Y + 10240, 2048)
            esB = Buf()
            op(DVE, f_memset(es_row[:, :], 0.0), [], [esB])
            op(ACT, f_act(es16[0:1, :], sinks[0:1, :], AF.Exp), [constB], [esB])
            for hh in range(16):
                op(DVE, f_ts(es_row[0:1, hh * 128:(hh + 1) * 128], ones[0:1, 0:128], es16[0:1, hh:hh + 1], None, ALU.mult),
                   [esB, constB], [esB])
            ohat3 = scr_bf(O0, 16384).rearrange("p (c t) -> p c t", c=8)
            def stage_a(i, n):
                kbs = [kb for kb in (i - 1, i, i + 1) if 0 <= kb < NQB]
                ptall = pt_pool.next()
                pt5 = ptall.ap.rearrange("p (k a b q) -> p k a b q", k=3, a=2, b=2)
                bank_of = {}
                for b in range(2):
                    bank_of[(b, 0)] = poolA.next()
                    if len(kbs) == 3:
                        bank_of[(b, 1)] = poolA.next()
                for idx, kb in enumerate(kbs):
                    for g in range(4):
                        hd = 4 * n + g
                        c = hd // 2
                        b = hd % 2
                        lo = b * 64
                        bk = bank_of[(b, idx // 2)]
                        col = ((idx % 2) * 2 + g // 2) * 128
                        op(PE, f_mm(bk.ap[:, col:col + 128],
                                    krot[n][lo:lo + 64, kb * 128:(kb + 1) * 128],
                                    qrot[c][lo:lo + 64, i * 128:(i + 1) * 128], True, True),
                           [kB[n][kb // 4], qB[c][i // 4]], [bk.buf])
                for half in range(2):
                    nk = min(2, len(kbs) - 2 * half)
                    if nk <= 0:
                        continue
                    for b in range(2):
                        bk = bank_of[(b, half)]
                        src = bk.ap[:, 0:nk * 256].rearrange("p (k a q) -> p k a q", k=nk, a=2)
                        op(ACT, f_act(pt5[:, 2 * half:2 * half + nk, :, b, :], src, AF.Exp, scale=0.125),
                           [bk.buf], [ptall.buf])
                pts = []
                for idx, kb in enumerate(kbs):
                    pt_ap = ptall.ap[:, idx * 512:(idx + 1) * 512]
                    if kb == i - 1:
                        op(DVE, f_tt(pt_ap, pt_ap, mlo4[:, :], ALU.mult), [ptall.buf, constB], [ptall.buf])
                    elif kb == i + 1:
                        op(DVE, f_tt(pt_ap, pt_ap, mhi4[:, :], ALU.mult), [ptall.buf, constB], [ptall.buf])
                    pts.append((kb, pt_ap))
                return (i, n, pts, ptall)

            def stage_b(state):
                i, n, pts, ptall = state
                od = poolB.next()
                for idx, (kb, pt_ap) in enumerate(pts):
                    op(PE, f_mm(od.ap[:, :], vaug(kb, n), pt_ap, idx == 0, False), [vB[kb], ptall.buf], [od.buf])
                op(PE, f_mm(od.ap[:, :], zo[:, :], es_row[:, n * 512:(n + 1) * 512], False, True),
                   [esB, constB], [od.buf])
                lr = lr_pool.next()
                op(ACT, f_act(lr.ap[64:128, :], od.ap[64:128, :], AF.Ln), [od.buf], [lr.buf])
                op(ACT, f_act(lr.ap[64:128, :], lr.ap[64:128, :], AF.Exp, scale=-1.0), [lr.buf], [lr.buf])
                o4 = od.ap[0:64, :].rearrange("p (a b q) -> p a b q", a=2, b=2)
                r4 = lr.ap[64:128, :].rearrange("p (a b q) -> p a b q", a=2, b=2)
                for b in range(2):
                    dsto = ohat3[b * 64:(b + 1) * 64, 2 * n:2 * n + 2, i * 128:(i + 1) * 128]
                    op(DVE, f_tt(dsto, o4[:, :, b, :], r4[:, :, b, :], ALU.mult), [od.buf, lr.buf],
                       [oB[2 * n][i], oB[2 * n + 1][i]])

            items = [(i, n) for i in range(NQB) for n in range(4)]
            prev = stage_a(*items[0])
            for it in items[1:]:
                cur = stage_a(*it)
                stage_b(prev)
                prev = cur
            stage_b(prev)
            if stop_after == "attn":
                return dict(o=ohat)

            barrier()
            xvb = make_x(scr_f32(0, 8192))
            for st in range(2):
                mov = [ohat[k][:, st * 1024:(st + 1) * 1024] for k in range(8)]
                movB = []
                for k in range(8):
                    bb = Buf()
                    toks = [oB[k][i].w for i in range(st * 8, st * 8 + 8)]
                    bb.w = max(toks, key=lambda t: t[1])
                    movB.append(bb)
                proj_postnorm(lambda m: [(ring_load(wo[m], 1024), 8)], 8, mov, movB, xvb, 0, 1, st)
            return None

        def conv_layer():
            l = 1
            barrier()
            xvb = make_x(scr_f32(0, 8192))
            H0 = 32768
            hn = [scr_bf(H0 + c * 2056, 1026) for c in range(NCH)]
            hnB = [Buf() for _ in range(NCH)]
            A0 = H0 + 16448
            yb = [scr_bf(A0 + m * 2048, 1024) for m in range(8)]
            ybB = [Buf() for _ in range(8)]
            T0 = A0 + 16384
            xs_pool = Pool_([Item(scr_f32(T0 + i * 4112, 1026)) for i in range(2)])
            cx_pool = Pool_([Item(scr_f32(T0 + 8224 + i * 4112, 1026)) for i in range(2)])
            y_pool = Pool_([Item(scr_f32(T0 + 16448 + i * 4096, 1024)) for i in range(2)])

            def in_loop(st, hc):
                zc = 1025 - hc
                lag = []

                def finish(item):
                    m, y, sb = item
                    b0, b1, _ = proj_halo(sb, hn, hnB, hc, with_halo=False)
                    op(DVE, f_tt(yb[m][:, 0:512], y.ap[:, 0:512], b0.ap[:, :], ALU.mult), [y.buf, b0.buf], [ybB[m]])
                    op(DVE, f_tt(yb[m][:, 512:1024], y.ap[:, 512:1024], b1.ap[:, :], ALU.mult), [y.buf, b1.buf], [ybB[m]])

                for m in range(8):
                    sx = ring_load(win[16 + m], 1024)
                    sc = ring_load(win[8 + m], 1024)
                    x0, x1, xh = proj_halo(sx, hn, hnB, hc, with_halo=(st == 0))
                    xs = xs_pool.next()
                    op(ACT, f_act(xs.ap[:, 1:513], x0.ap[:, :], AF.Copy), [x0.buf], [xs.buf])
                    op(ACT, f_act(xs.ap[:, 513:1025], x1.ap[:, :], AF.Copy), [x1.buf], [xs.buf])
                    if st == 0:
                        op(ACT, f_act(xs.ap[:, hc:hc + 1], xh.ap, AF.Copy), [xh.buf], [xs.buf])
                    c0, c1, ch = proj_halo(sc, hn, hnB, hc, with_halo=(st == 0))
                    if lag:
                        finish(lag.pop(0))
                    sb = ring_load(win[m], 1024)
                    cx = cx_pool.next()
                    op(DVE, f_tt(cx.ap[:, 1:513], c0.ap[:, :], xs.ap[:, 1:513], ALU.mult), [c0.buf, xs.buf], [cx.buf])
                    op(DVE, f_tt(cx.ap[:, 513:1025], c1.ap[:, :], xs.ap[:, 513:1025], ALU.mult), [c1.buf, xs.buf], [cx.buf])
                    if st == 0:
                        op(DVE, f_tt(cx.ap[:, hc:hc + 1], ch.ap, xs.ap[:, hc:hc + 1], ALU.mult), [ch.buf, xs.buf], [cx.buf])
                        op(DVE, f_copy(hsave[:, 24 + m:25 + m], cx.ap[:, 1024:1025]), [cx.buf], [hsaveB])
                    else:
                        op(DVE, f_copy(cx.ap[:, 0:1], hsave[:, 24 + m:25 + m]), [hsaveB], [cx.buf])
                    op(DVE, f_memset(cx.ap[:, zc:zc + 1], 0.0), [], [cx.buf])
                    y = y_pool.next()
                    conv3(y.ap, cx.ap, lambda jj, m=m: cst[:, C_CW + jj * 8 + m: C_CW + jj * 8 + m + 1], cx.buf, y.buf)
                    lag.append((m, y, sb))
                    drain(4)
                while lag:
                    finish(lag.pop(0))

            ufn = lambda m: [(ring_load(wout[m], 1024), 8)]
            hc = norm_super(l, 0, 0, hn, hnB)
            in_loop(0, hc)
            hc1 = [0]
            proj_postnorm(ufn, 8, yb, ybB, xvb, 1, 1, 0,
                          mid_hook=lambda: hc1.__setitem__(0, norm_super(l, 0, 1, hn, hnB, ssp=poolA)), defer=True)
            in_loop(1, hc1[0])
            proj_postnorm(ufn, 8, yb, ybB, xvb, 1, 1, 1)

        dma(SP, f_dma(cst[:, :], cstd[:, :]), [], [cstB], misc_sem)
        dma(SP, f_dma(sinks[0:1, :], sinkd[:, :]), [], [constB], misc_sem)
        op(DVE, f_memset(ones[:, :], 1.0), [], [constB])
        op(DVE, f_memset(epst[:, :], EPS), [], [constB])
        op(DVE, f_memset(zo[:, :], 0.0), [], [constB])
        op(DVE, f_memset(zo[0:1, 64:128], 1.0), [constB], [constB])
        op(DVE, f_copy(pm[:, :], cst[:, C_PM:C_PM + 128]), [cstB], [constB])
        for g in range(4):
            op(DVE, f_copy(mlo4[:, g * 128:(g + 1) * 128], cst[:, C_MLO:C_MLO + 128]), [cstB], [constB])
            op(DVE, f_copy(mhi4[:, g * 128:(g + 1) * 128], cst[:, C_MHI:C_MHI + 128]), [cstB], [constB])

        dump = None

        def load_half(seq, hf):
            for c in range(NCH):
                dma(SP, f_dma(h[:, c, hf * 1024:(hf + 1) * 1024], xT[seq, c * 128:(c + 1) * 128, hf * 1024:(hf + 1) * 1024]),
                    [], [hB[c][2 * hf], hB[c][2 * hf + 1]], ld_sems[c][hf])

        def store_half(seq, hf):
            for c in range(NCH):
                dma(SP, f_dma(yT[seq, c * 128:(c + 1) * 128, hf * 1024:(hf + 1) * 1024], h[:, c, hf * 1024:(hf + 1) * 1024]),
                    [hB[c][2 * hf], hB[c][2 * hf + 1]], [], st_sems[c][hf])

        early = [False]
        for seq in range(n_seq):
            if not early[0]:
                load_half(seq, 0)
            load_half(seq, 1)
            early[0] = False
            dump = attn_layer(seq)
            full_run = False
            if dump is not None:
                pass
            elif stop_after == "l0mix":
                pass
            else:
                ffn_phase(0)
                if stop_after != "l0":
                    conv_layer()
                    if stop_after != "l1mix":
                        full_run = True

                        def hook(seq=seq):
                            store_half(seq, 0)
                            if seq + 1 < n_seq:
                                load_half(seq + 1, 0)
                                early[0] = True
                        ffn_phase(1, after_gu1=hook)
            if not full_run:
                store_half(seq, 0)
            store_half(seq, 1)

        if dump and dbg is not None:
            barrier()
            col = 0
            for name, aps in dump.items():
                aps = aps if isinstance(aps, list) else [aps]
                for a0 in aps:
                    for c0 in range(0, a0.shape[-1], 2048):
                        a = a0[:, c0:min(c0 + 2048, a0.shape[-1])]
                        n = a.shape[-1]
                        stg = scr_f32(SCRB - 8192, 2048)[:, 0:n]
                        bb = Buf()
                        op(DVE, f_copy(stg, a), [], [bb])
                        dma(SP, f_dma(dbg[:, col:col + n], stg), [bb], [], dbg_sem)
                        barrier()
                        col += n
        barrier(final=True)

        with nc.Block() as block:
            @block.tensor
            def _(e):
                PE.replay(e)

            @block.scalar
            def _(e):
                ACT.replay(e)

            @block.vector
            def _(e):
                DVE.replay(e)

            @block.gpsimd
            def _(e):
                POOL.replay(e)

            @block.sync
            def _(e):
                SP.replay(e)
    return nc


def _units(W, cols_list):
    K = W.shape[0]
    nk = K // 128
    Wr = W.reshape(nk, 128, W.shape[1])
    out = np.empty((len(cols_list), 128, nk * 128), np.float32)
    for i, ci in enumerate(cols_list):
        out[i] = Wr[:, :, ci].transpose(1, 0, 2).reshape(128, nk * 128)
    return out


def prep_shared(inp):
    f = lambda a: np.asarray(a, dtype=np.float32)
    wqkv = f(inp["attn_w_qkv"])[0]
    ar = np.arange(128)
    cols = [m * 128 + ar for m in range(8)] + [1024 + n * 64 + (ar % 64) for n in range(4)]
    sh = {}
    sh["wqk"] = _units(wqkv, cols)
    Wr = wqkv.reshape(8, 128, 1536)
    sh["wv"] = np.stack([Wr[4 * u:4 * u + 4, :, 1280:1536].transpose(1, 0, 2).reshape(128, 1024) for u in range(2)])
    sh["wo"] = _units(f(inp["attn_w_o"])[0], [m * 128 + ar for m in range(8)])
    sh["win"] = _units(f(inp["conv_w_in"])[0], [m * 128 + ar for m in range(24)])
    sh["wout"] = _units(f(inp["conv_w_out"])[0], [m * 128 + ar for m in range(8)])
    gu = f(inp["ffn_w_gate_up"])
    cols = []
    for j in range(NFF):
        cols.append(j * 128 + ar)
        cols.append(DFF + j * 128 + ar)
    sh["wgu"] = np.stack([_units(gu[l], cols) for l in range(2)])
    wdn = f(inp["ffn_w_down"])
    sh["wd"] = np.stack([_units(wdn[l], [m * 128 + ar for m in range(8)]) for l in range(2)])
    cst = np.zeros((128, C_TOT), np.float32)
    ng = f(inp["norm_gains"])
    for l in range(2):
        for j in range(4):
            cst[:, C_G + (l * 4 + j) * 8: C_G + (l * 4 + j) * 8 + 8] = ng[l, j].reshape(8, 128).T
    cw = f(inp["conv_w"])[0]
    for j in range(3):
        cst[:, C_CW + j * 8: C_CW + j * 8 + 8] = cw[j].reshape(8, 128).T
    fcw = f(inp["ffn_conv_w"])
    for l in range(2):
        for j in range(3):
            cst[:, C_FCW + (l * 3 + j) * 22: C_FCW + (l * 3 + j) * 22 + 22] = fcw[l, j].reshape(22, 128).T
    inv_freq = (10000.0 ** (-np.arange(0, 64, 2, dtype=np.float32) / np.float32(64))).astype(np.float32)
    p = np.arange(128)
    d = p % 64
    cst[:, C_INVF] = inv_freq[d % 32]
    cst[:, C_SIGN] = np.where(d < 32, -1.0, 1.0)
    partner = np.where(d < 32, p + 32, p - 32)
    cst[partner, C_PM + p] = 1.0
    r = p[:, None]
    t = p[None, :]
    cst[:, C_MLO:C_MLO + 128] = (t <= r).astype(np.float32)
    cst[:, C_MHI:C_MHI + 128] = (r <= t).astype(np.float32)
    sh["cst"] = cst
    sh["sink"] = f(inp["attn_sink"]).reshape(1, 16)
    sh["pos"] = np.ascontiguousarray(np.broadcast_to(np.asarray(inp["positions"], dtype=np.int32)[None, :], (128, S)))
    return sh


_NC_CACHE = {}


def kernel(**inputs):
    x = np.asarray(inputs["x"], dtype=np.float32)
    sh = prep_shared(inputs)
    n_cores = 8
    per = x.shape[0] // n_cores
    if "nc" not in _NC_CACHE:
        _NC_CACHE["nc"] = build_program(n_seq=per)
    nc = _NC_CACHE["nc"]
    in_maps = []
    for c in range(n_cores):
        m = dict(sh)
        m["xT"] = np.ascontiguousarray(x[c * per:(c + 1) * per].transpose(0, 2, 1))
        in_maps.append(m)
    res = run_bass_kernel_spmd(nc, in_maps, core_ids=list(range(n_cores)))
    out = np.empty_like(x)
    for c in range(n_cores):
        out[c * per:(c + 1) * per] = res.results[c]["yT"].transpose(0, 2, 1)
    return out
```
